# Optimizing a Trainium2 kernel written in Bass

```python
import math
import jax, jax.numpy as jnp
from jax import lax
import numpy as np

D_MODEL = 1024
BATCH = 1
SEQ = 16384
DEPTH = 4
DEC_BATCH = 16
DEC_SEQ = 64
PAST_LEN = 4096

CHUNK = 64
Q_BLOCK = 128
MIX_WIDTH = D_MODEL
GLA_WIDTH = MIX_WIDTH // 2
GLA_HEADS = 4
GLA_DV = GLA_WIDTH // GLA_HEADS
GLA_DK = GLA_DV // 2
GLA_GATE_RANK = 16
GLA_GATE_NORMALIZER = 16.0
DIFF_WIDTH = MIX_WIDTH - GLA_WIDTH
DIFF_HEADS = 4
DIFF_DV = DIFF_WIDTH // DIFF_HEADS
DIFF_HEAD_DIM = DIFF_DV // 2
DIFF_KDIM = 2 * DIFF_HEAD_DIM
T5_BUCKETS = 32
T5_MAX_DIST = 128
D_FF = ((8 * D_MODEL // 3 + 255) // 256) * 256
NEG_INF = -1e30
RMS_EPS = 1e-6
IN_SPLITS = (GLA_HEADS * GLA_DK, GLA_HEADS * GLA_DK, GLA_WIDTH, GLA_WIDTH, GLA_GATE_RANK,
             DIFF_HEADS * DIFF_KDIM, DIFF_HEADS * DIFF_KDIM, DIFF_WIDTH)
D_IN = 2 * GLA_HEADS * GLA_DK + 2 * GLA_WIDTH + GLA_GATE_RANK + 2 * DIFF_HEADS * DIFF_KDIM + DIFF_WIDTH

kernel_name = 'hymba_gla_diffattn_streaming_step'


def split_points():
    pts, acc = [], 0
    for s in IN_SPLITS[:-1]:
        acc += s
        pts.append(acc)
    return pts


def rmsnorm(x, g):
    xf = x.astype(jnp.float32)
    y = xf * lax.rsqrt(jnp.mean(xf * xf, axis=-1, keepdims=True) + RMS_EPS)
    return (y * g.astype(jnp.float32)).astype(x.dtype)


def t5_bucket(rel):
    nb = T5_BUCKETS // 2
    max_exact = nb // 2
    ret = jnp.where(rel > 0, nb, 0)
    n = jnp.abs(rel)
    nf = jnp.maximum(n, 1).astype(jnp.float32)
    large = max_exact + (jnp.log(nf / max_exact) / math.log(T5_MAX_DIST / max_exact)
                         * (nb - max_exact)).astype(jnp.int32)
    large = jnp.minimum(large, nb - 1)
    return ret + jnp.where(n < max_exact, n, large)


def gla_chunk(S, q, k, v, log_a):
    L = q.shape[1]
    b = jnp.cumsum(log_a.astype(jnp.float32), axis=1)
    causal = jnp.tril(jnp.ones((L, L), dtype=bool))
    rel = b[:, :, None] - b[:, None, :]
    decay = jnp.exp(jnp.where(causal[None, :, :, None, None], rel, NEG_INF))
    scores = jnp.einsum('bthd,bshd,btshd->bhts', q, k, decay)
    o_intra = jnp.einsum('bhts,bshv->bthv', scores, v)
    o_inter = jnp.einsum('bthd,bhdv->bthv', q * jnp.exp(b), S)
    b_last = b[:, -1]
    k_dec = k * jnp.exp(b_last[:, None] - b)
    S_new = jnp.exp(b_last)[..., None] * S + jnp.einsum('bshd,bshv->bhdv', k_dec, v)
    return S_new.astype(S.dtype), (o_intra + o_inter).astype(jnp.float32)


def gla_prompt(q, k, v, log_a):
    B, T = q.shape[:2]
    n = T // CHUNK

    def to_chunks(a):
        return jnp.moveaxis(a.reshape(B, n, CHUNK, *a.shape[2:]), 1, 0)

    S0 = jnp.zeros((B, GLA_HEADS, GLA_DK, GLA_DV), jnp.float32)
    S_fin, o = lax.scan(lambda S, c: gla_chunk(S, *c), S0,
                        (to_chunks(q), to_chunks(k), to_chunks(v), to_chunks(log_a)))
    o = jnp.moveaxis(o, 0, 1).reshape(B, T, GLA_HEADS, GLA_DV)
    return o, S_fin


def diff_attend(q, k, v, qpos, kpos, lam, rel_bias):
    logits = jnp.einsum('bqhmd,bkhmd->bhmqk', q, k).astype(jnp.float32) * (DIFF_HEAD_DIM ** -0.5)
    bias = jnp.transpose(rel_bias[t5_bucket(kpos[None, :] - qpos[:, None])].astype(jnp.float32), (2, 0, 1))
    allowed = (kpos[None, :] // CHUNK) <= (qpos[:, None] // CHUNK)
    logits = jnp.where(allowed, logits + bias[None, :, None], NEG_INF)
    p = jax.nn.softmax(logits, axis=-1)
    a = p[:, :, 0] - lam * p[:, :, 1]
    return jnp.einsum('bhqk,bkhv->bqhv', a, v.astype(jnp.float32))


def diff_prompt(q, k, v, lam, rel_bias):
    B, T = q.shape[:2]
    nb = T // Q_BLOCK
    qb = jnp.moveaxis(q.reshape(B, nb, Q_BLOCK, *q.shape[2:]), 1, 0)
    starts = jnp.arange(nb, dtype=jnp.int32) * Q_BLOCK
    kpos = jnp.arange(T, dtype=jnp.int32)

    def block(args):
        qblk, s = args
        return diff_attend(qblk, k, v, s + jnp.arange(Q_BLOCK, dtype=jnp.int32), kpos, lam, rel_bias)

    o = lax.map(block, (qb, starts))
    return jnp.moveaxis(o, 0, 1).reshape(B, T, DIFF_HEADS, DIFF_DV)


def run_trunk(x, past_k, past_v, gla_state, w_in, gla_w_alpha2, gla_b_alpha, gla_norm_g,
              diff_lambda, diff_norm_g, w_out, norm_mix_g, norm_ffn_g, w_ffn_in, w_ffn_out,
              final_norm_g, rel_bias):
    B, T, _ = x.shape
    new_k, new_v, new_s = [], [], []
    for l in range(DEPTH):
        xn = rmsnorm(x, norm_mix_g[l])
        gq, gk, gv, gg, glr, dq, dk, dv = jnp.split(xn @ w_in[l], split_points(), axis=-1)
        q = gq.reshape(B, T, GLA_HEADS, GLA_DK) * (GLA_DK ** -0.5)
        k = gk.reshape(B, T, GLA_HEADS, GLA_DK)
        v = gv.reshape(B, T, GLA_HEADS, GLA_DV)
        z = (glr @ gla_w_alpha2[l] + gla_b_alpha[l]).astype(jnp.float32)
        log_a = (jax.nn.log_sigmoid(z) / GLA_GATE_NORMALIZER).reshape(B, T, GLA_HEADS, GLA_DK)
        if gla_state is None:
            o_gla, s_fin = gla_prompt(q, k, v, log_a)
        else:
            s_fin, o_gla = gla_chunk(gla_state[l], q, k, v, log_a)
        o_gla = rmsnorm(o_gla, gla_norm_g[l]).reshape(B, T, GLA_WIDTH) * jax.nn.silu(gg.astype(jnp.float32))
        lam_init = 0.8 - 0.6 * math.exp(-0.3 * l)
        lp = diff_lambda[l].astype(jnp.float32)
        lam = jnp.exp(jnp.sum(lp[0] * lp[1])) - jnp.exp(jnp.sum(lp[2] * lp[3])) + lam_init
        qd = dq.reshape(B, T, DIFF_HEADS, 2, DIFF_HEAD_DIM)
        kd = dk.reshape(B, T, DIFF_HEADS, 2, DIFF_HEAD_DIM)
        vd = dv.reshape(B, T, DIFF_HEADS, DIFF_DV)
        if past_k is None:
            o_diff = diff_prompt(qd, kd, vd, lam, rel_bias)
        else:
            P = past_k.shape[2]
            k_all = jnp.concatenate([past_k[l].reshape(B, P, DIFF_HEADS, 2, DIFF_HEAD_DIM), kd], axis=1)
            v_all = jnp.concatenate([past_v[l], vd], axis=1)
            qpos = P + jnp.arange(T, dtype=jnp.int32)
            kpos = jnp.arange(P + T, dtype=jnp.int32)
            o_diff = diff_attend(qd, k_all, v_all, qpos, kpos, lam, rel_bias)
        o_diff = rmsnorm(o_diff, diff_norm_g[l]).reshape(B, T, DIFF_WIDTH) * (1.0 - lam_init)
        x = x + jnp.concatenate([o_gla, o_diff], axis=-1).astype(x.dtype) @ w_out[l]
        hn = rmsnorm(x, norm_ffn_g[l])
        gate, up = jnp.split(hn @ w_ffn_in[l], 2, axis=-1)
        x = x + (jax.nn.silu(gate) * up) @ w_ffn_out[l]
        new_k.append(dk.reshape(B, T, DIFF_HEADS, DIFF_KDIM))
        new_v.append(vd)
        new_s.append(s_fin)
    y = rmsnorm(x, final_norm_g)
    return y, jnp.stack(new_k), jnp.stack(new_v), jnp.stack(new_s)


def setup_inputs(seed: int = 0) -> dict:
    key = jax.random.key(seed)
    ks = jax.random.split(key, 18)
    f32 = jnp.float32

    def nrm(k, shape, scale):
        return jax.random.normal(k, shape, f32) * scale

    return {
        'x_prompt': nrm(ks[0], (BATCH, SEQ, D_MODEL), 1.0),
        'x_sample': nrm(ks[1], (DEC_BATCH, DEC_SEQ, D_MODEL), 1.0),
        'cache_diff_k': nrm(ks[2], (DEPTH, DEC_BATCH, PAST_LEN, DIFF_HEADS, DIFF_KDIM), 1.0),
        'cache_diff_v': nrm(ks[3], (DEPTH, DEC_BATCH, PAST_LEN, DIFF_HEADS, DIFF_DV), 1.0),
        'state_gla': nrm(ks[4], (DEPTH, DEC_BATCH, GLA_HEADS, GLA_DK, GLA_DV), 0.5),
        'w_in': nrm(ks[5], (DEPTH, D_MODEL, D_IN), D_MODEL ** -0.5),
        'gla_w_alpha2': nrm(ks[6], (DEPTH, GLA_GATE_RANK, GLA_HEADS * GLA_DK), GLA_GATE_RANK ** -0.5),
        'gla_b_alpha': nrm(ks[7], (DEPTH, GLA_HEADS * GLA_DK), 0.1),
        'gla_norm_g': 1.0 + nrm(ks[8], (DEPTH, GLA_DV), 0.02),
        'diff_lambda': nrm(ks[9], (DEPTH, 4, DIFF_HEAD_DIM), 0.1),
        'diff_norm_g': 1.0 + nrm(ks[10], (DEPTH, DIFF_DV), 0.02),
        'w_out': nrm(ks[11], (DEPTH, MIX_WIDTH, D_MODEL), MIX_WIDTH ** -0.5),
        'norm_mix_g': 1.0 + nrm(ks[12], (DEPTH, D_MODEL), 0.02),
        'norm_ffn_g': 1.0 + nrm(ks[13], (DEPTH, D_MODEL), 0.02),
        'w_ffn_in': nrm(ks[14], (DEPTH, D_MODEL, 2 * D_FF), D_MODEL ** -0.5),
        'w_ffn_out': nrm(ks[15], (DEPTH, D_FF, D_MODEL), D_FF ** -0.5),
        'final_norm_g': 1.0 + nrm(ks[16], (D_MODEL,), 0.02),
        'rel_bias': nrm(ks[17], (T5_BUCKETS, DIFF_HEADS), 0.5),
    }


def reference(x_prompt, x_sample, cache_diff_k, cache_diff_v, state_gla, w_in, gla_w_alpha2,
              gla_b_alpha, gla_norm_g, diff_lambda, diff_norm_g, w_out, norm_mix_g, norm_ffn_g,
              w_ffn_in, w_ffn_out, final_norm_g, rel_bias):
    y_prompt, new_k_prompt, new_v_prompt, new_gla_prompt = run_trunk(
        x_prompt, None, None, None, w_in, gla_w_alpha2, gla_b_alpha, gla_norm_g, diff_lambda,
        diff_norm_g, w_out, norm_mix_g, norm_ffn_g, w_ffn_in, w_ffn_out, final_norm_g, rel_bias)
    y_sample, new_k_sample, new_v_sample, new_gla_sample = run_trunk(
        x_sample, cache_diff_k, cache_diff_v, state_gla, w_in, gla_w_alpha2, gla_b_alpha, gla_norm_g,
        diff_lambda, diff_norm_g, w_out, norm_mix_g, norm_ffn_g, w_ffn_in, w_ffn_out, final_norm_g, rel_bias)
    return (y_prompt, y_sample, new_k_prompt, new_v_prompt, new_gla_prompt,
            new_k_sample, new_v_sample, new_gla_sample)
```

```python
import math
from contextlib import ExitStack

import numpy as np
import ml_dtypes

import concourse.bass as bass
import concourse.mybir as mybir
from concourse.bass_utils import run_bass_kernel_spmd

F32 = mybir.dt.float32
BF16 = mybir.dt.bfloat16
AF = mybir.ActivationFunctionType
ALU = mybir.AluOpType

NCORE = 8
D = 1024
KC = 8
DIN = 3088
DFF = 2816
NFF = 22
NEG = -1.0e30
EPS = 1e-6
C_GQ, C_GK, C_GV, C_GG, C_GLR, C_DQ, C_DK, C_DV = 0, 256, 512, 1024, 1536, 1552, 2064, 2576


class Op:
    __slots__ = ("eng", "fn", "deps", "dma", "needed", "sem", "val", "coll")

    def __init__(self, eng, fn, dma, coll=False):
        self.eng, self.fn, self.dma, self.coll = eng, fn, dma, coll
        self.deps = []
        self.needed = False
        self.sem = None
        self.val = 0


class Sched:
    ENGS = ("pe", "act", "dve", "pool", "sp")

    def __init__(self, n_dma_sems=20):
        self.ops = {e: [] for e in self.ENGS}
        self.lastw = {}
        self.readers = {}
        self.n_dma_sems = n_dma_sems
        self.dma_hist = {e: [] for e in self.ENGS}
        self.all_dma = []

    def add(self, eng, fn, reads=(), writes=(), dma=False, coll=False):
        op = Op(eng, fn, dma, coll)
        deps = {}
        for k in reads:
            w = self.lastw.get(k)
            if w is not None:
                deps[id(w)] = w
        for k in writes:
            w = self.lastw.get(k)
            if w is not None:
                deps[id(w)] = w
            for r in self.readers.get(k, ()):
                deps[id(r)] = r
        if dma and not coll:
            h = self.dma_hist[eng]
            if len(h) >= self.n_dma_sems:
                p = h[-self.n_dma_sems]
                deps[id(p)] = p
            h.append(op)
        for d in deps.values():
            if d is op:
                continue
            if d.eng == "pe" and eng == "pe" and not d.dma and not dma:
                continue
            op.deps.append(d)
            d.needed = True
        for k in reads:
            self.readers.setdefault(k, []).append(op)
        for k in writes:
            self.lastw[k] = op
            self.readers[k] = []
        self.ops[eng].append(op)
        if dma:
            self.all_dma.append(op)
        return op

    def emit(self, nc, stack):
        esem = {e: stack.enter_context(nc.semaphore("es_" + e)) for e in self.ENGS}
        dsem = {e: [stack.enter_context(nc.semaphore("ds_%s_%d" % (e, i))) for i in range(self.n_dma_sems)]
                for e in ("sp", "pool")}
        ncoll = sum(1 for o in self.all_dma if o.coll)
        csem = [stack.enter_context(nc.semaphore("cs_%d" % i)) for i in range(ncoll)]
        fin = Op("sp", None, False)
        fin.deps = list(self.all_dma)
        for d in fin.deps:
            d.needed = True
        self.ops["sp"].append(fin)
        ci = 0
        for e in self.ENGS:
            cnt = 0
            dcnt = [0] * self.n_dma_sems
            di = 0
            for op in self.ops[e]:
                if op.coll:
                    op.sem, op.val = csem[ci], 1
                    ci += 1
                elif op.dma:
                    op.sem = dsem[e][di]
                    dcnt[di] += 16
                    op.val = dcnt[di]
                    di = (di + 1) % self.n_dma_sems
                elif op.needed:
                    cnt += 1
                    op.sem, op.val = esem[e], cnt
        handles = {"pe": nc.tensor, "act": nc.scalar, "dve": nc.vector, "pool": nc.gpsimd, "sp": nc.sync}

        def run(e):
            h = handles[e]
            seen = {}
            for op in self.ops[e]:
                for d in op.deps:
                    key = id(d.sem)
                    if seen.get(key, 0) < d.val:
                        h.wait_ge(d.sem, d.val)
                        seen[key] = d.val
                if op.fn is None:
                    continue
                ins = op.fn(h)
                if op.coll:
                    ins.then_inc(op.sem)
                elif op.dma:
                    ins.then_inc(op.sem, 16)
                elif op.needed:
                    ins.then_inc(op.sem, 1)

        with nc.Block() as block:
            @block.tensor
            def _(_e):
                run("pe")

            @block.scalar
            def _(_e):
                run("act")

            @block.vector
            def _(_e):
                run("dve")

            @block.gpsimd
            def _(_e):
                run("pool")

            @block.sync
            def _(_e):
                run("sp")


def _t5_bucket(rel):
    rel = np.asarray(rel, np.int64)
    nb, max_exact = 16, 8
    ret = np.where(rel > 0, nb, 0)
    n = np.abs(rel)
    nf = np.maximum(n, 1).astype(np.float32)
    v = (np.log(nf / np.float32(max_exact)) / np.float32(math.log(128 / max_exact))).astype(np.float32)
    large = max_exact + (v * np.float32(nb - max_exact)).astype(np.int32)
    large = np.minimum(large, nb - 1)
    return ret + np.where(n < max_exact, n, large)


def _host_consts(core):
    q = np.arange(128)[:, None]
    k = np.arange(128)[None, :]
    idx = np.zeros((128, 2, 128), np.float32)
    bd = _t5_bucket(k - q).astype(np.float32)
    allowed = (k // 64) <= (q // 64)
    idx[:, 0, :] = np.where(allowed, bd, 32.0)
    idx[:, 1, :] = _t5_bucket(k - 128 - q).astype(np.float32)
    ident = np.eye(128, dtype=np.float32).astype(ml_dtypes.bfloat16)
    s = np.arange(128)[:, None]
    t = np.arange(128)[None, :]
    tri_p = np.where(s <= t, -1.0 / 16.0, 0.0).astype(np.float32)
    tri_s = np.where((s <= t) & ((s // 64) == (t // 64)), -1.0 / 16.0, 0.0).astype(np.float32)
    cm_p = np.repeat((s <= t).astype(np.float32)[:, None, :], 4, axis=1).astype(ml_dtypes.bfloat16)
    cm_s = np.repeat(((s <= t) & ((s // 64) == (t // 64))).astype(np.float32)[:, None, :], 4, axis=1).astype(ml_dtypes.bfloat16)
    sc = np.zeros((128, 12, 3), np.float32)
    for si, tt in enumerate(range(-4, 8)):
        r = tt - core
        if r > 0:
            sc[:, si, 2] = 1.0
        elif r == 0:
            sc[:, si, 0] = 1.0
        elif r == -1:
            sc[:, si, 1] = 1.0
    sel = np.zeros((128, 8), np.float32)
    sel[:, core] = 1.0
    zm = np.zeros((128, 2, 128), np.float32)
    zm[:, 0, :64] = 1.0
    zm[:, 1, 64:] = 1.0
    return dict(c_idx=idx, c_ident=ident, c_trip=tri_p, c_tris=tri_s, c_cmp=cm_p, c_cms=cm_s,
                c_slot=sc.reshape(128, 36), c_sel=sel, c_zm=zm.astype(ml_dtypes.bfloat16))


GT = 4
VW = 130
NSEQ = 16
STOP = 99
DEBUG = False


class StopBuild(Exception):
    pass


def _stop(level):
    if STOP <= level:
        raise StopBuild()


def build(NPT, PB, L):
    SEQ = NPT * 128
    NROW = SEQ + NSEQ * 64
    P = PB * 128
    nc = bass.Bass("TRN2", target_bir_lowering=False)
    S = Sched()
    st = ExitStack()

    def din(name, shape, dt=F32):
        return nc.dram_tensor(name, list(shape), dt, kind="ExternalInput")

    def dout(name, shape, dt=F32):
        return nc.dram_tensor(name, list(shape), dt, kind="ExternalOutput")

    xin = din("xin", [NROW, D])
    w_in = din("w_in", [L, D, DIN])
    w_out = din("w_out", [L, D, D])
    w_f1 = din("w_f1", [L, D, 2 * DFF])
    w_f2 = din("w_f2", [L, DFF, D])
    w2a = din("w2a", [L, 17, 256])
    g_mix = din("g_mix", [L, D])
    g_ffn = din("g_ffn", [L, D])
    g_fin = din("g_fin", [1, D])
    g_gla = din("g_gla", [L, 128])
    g_dif = din("g_dif", [L, 128])
    lam_in = din("lam_in", [L, 256])
    rb_in = din("rb_in", [1, 128])
    ck = din("ck", [L, NSEQ, 4, 128, P])
    cv = din("cv", [L, NSEQ, 4, 128, PB * 128])
    sg = din("sg", [L, NSEQ, 256, 128])
    c_idx = din("c_idx", [128, 2, 128])
    c_ident = din("c_ident", [128, 128], BF16)
    c_trip = din("c_trip", [128, 128])
    c_cmp = din("c_cmp", [128, 4, 128], BF16)

    y_o = dout("y_o", [NROW, D])
    nk_o = dout("nk_o", [L, NROW, 512])
    nv_o = dout("nv_o", [L, NROW, 512])
    ngp_o = dout("ngp_o", [L, 256, 128])
    ngs_o = dout("ngs_o", [L, NSEQ, 256, 128])
    dbg_o = dout("dbg_o", [2, 128, 8 * GT * 128], BF16) if DEBUG else None
    dbgx_o = dout("dbgx_o", [3, 128, GT, D]) if DEBUG else None

    KTd = [nc.dram_tensor("KTd%d" % l, [4, 128, NPT * 128], BF16) for l in range(L)]
    Vd = [nc.dram_tensor("Vd%d" % l, [4, 128, NPT, VW], BF16) for l in range(L)]

    def sb(name, shape, dt=F32):
        return st.enter_context(nc.sbuf_tensor(name, list(shape), dt))

    NG1 = 1552
    NG2 = 1536
    XTG_N = KC * GT * 128
    ACTB_N = NFF * GT * 128
    CKB_OFF = XTG_N + ACTB_N
    CVB_OFF = CKB_OFF + P
    RKN = max(NPT * 128 + NPT * VW, CVB_OFF + PB * VW)
    RWN = max(KC * NG1, NFF * 512 + 2 * KC * 256)

    X = sb("X", [128, GT, D])
    QT = sb("QT", [128, 4, 2, GT * 128], BF16)
    MT = sb("MT", [128, 8, GT * 128], BF16)
    RW = sb("RW", [128, RWN], BF16)
    RK = sb("RK", [128, RKN], BF16)
    BMS = sb("BMS", [128, 2, 4, 128], BF16)
    GB = sb("GB", [128, D])
    ident = sb("ident", [128, 128], BF16)
    ident2 = sb("ident2", [128, 2, 128], BF16)
    ident2s = sb("ident2s", [64, 2, 64], BF16)
    trip = sb("trip", [128, 128])
    cmp_ = sb("cmp", [128, 4, 128], BF16)
    rbb = sb("rbb", [128, 128])
    ch = sb("ch", [128, 4])
    lamv = sb("lamv", [128, L])
    nlam = sb("nlam", [128, L])
    ggl4 = sb("ggl4", [128, L, 4, 128])
    gdf = sb("gdf", [128, L * 128])
    w2s = sb("w2s", [17, L * 256], BF16)
    xn = sb("xn", [128, D], BF16)
    junk = sb("junk", [128, D])
    ssq = sb("ssq", [128, 8])
    rstd = sb("rstd", [128, 8])
    glrT = sb("glrT", [17, 128], BF16)
    lsp = sb("lsp", [128, 256])
    ebT = sb("ebT", [64, 4, 128])
    enbT = sb("enbT", [64, 4, 128])
    enb = sb("enb", [128, 256])
    ktT = sb("ktT", [64, 4, 128], BF16)
    qtT = sb("qtT", [64, 4, 128], BF16)
    ktok = sb("ktok", [128, 256], BF16)
    vtok = sb("vtok", [128, 512], BF16)
    scT = sb("scT", [128, 4, 128], BF16)
    gsl = sb("gsl", [128, 512])
    mixb = sb("mixb", [128, 512], BF16)
    udt = sb("udt", [64, 4, 128])
    Sst = sb("Sst", [64, L, 4, 128])
    Sb = sb("Sb", [64, L, 4, 128], BF16)
    S0 = sb("S0", [64, 4, 128])
    S0b = sb("S0b", [64, 4, 128], BF16)
    ev32 = [sb("ev32_%d" % i, [128, 512]) for i in range(2)]
    vaug = sb("vaug", [128, 4, VW], BF16)
    kts = sb("kts", [128, 4, 128], BF16)
    eT = [sb("eT%d" % i, [128, 1024], BF16) for i in range(2)]
    osb = sb("osb", [128, 2, VW])
    rs = sb("rs", [128, 4])
    od = sb("od", [128, 128])
    odb = sb("odb", [128, 128], BF16)
    ktsS = sb("ktsS", [128, GT, 4, 64], BF16)
    vaugS = sb("vaugS", [64, GT, 4, VW], BF16)

    PSL = st.enter_context(nc.psum_tensor("PSL", [128, 2048], F32))
    ACC = st.enter_context(nc.psum_tensor("ACC", [128, 1536], F32))
    PT = st.enter_context(nc.psum_tensor("PT", [128, 1024], BF16))
    BANK = [PSL[:, q * 512:(q + 1) * 512] for q in range(4)] + [ACC[:, q * 512:(q + 1) * 512] for q in range(3)]
    ACCKEYS = [("bank", 4), ("bank", 5), ("bank", 6)]
    psrr = [0]

    def nextps():
        i = psrr[0]
        psrr[0] = (i + 1) % 7
        return i

    def acc_ap(a, NQ):
        return ACC[0:NQ, (a // 3) * 512 + (a % 3) * VW:(a // 3) * 512 + (a % 3 + 1) * VW], a // 3

    idxv = RK[:, 0:512].bitcast(F32).rearrange("p (a k) -> p a k", a=2)
    tmpb = RK[:, 512:1024].bitcast(F32)
    lamt = RK[:, 1024:1024 + L * 512].bitcast(F32)
    b2o = 1024 + L * 512
    b2 = RK[:, b2o:b2o + 2048].bitcast(F32).rearrange("p (a h k) -> p a h k", a=2, h=4)
    gtmp = RK[:, b2o + 2048:b2o + 2048 + L * 256].bitcast(F32)

    XTg = RK[:, 0:XTG_N].rearrange("p (k t) -> p k t", k=KC)
    ACTB = RK[:, XTG_N:XTG_N + ACTB_N].rearrange("p (j t) -> p j t", j=NFF)
    ckb = RK[:, CKB_OFF:CKB_OFF + P]
    cvb = RK[:, CVB_OFF:CVB_OFF + PB * VW].rearrange("p (j c) -> p j c", c=VW)
    RKALL = ["RK", "actb", "ckb", "cvb"] + [("xtg", t) for t in range(GT)]
    RWALL = ["RW", "w2h", ("slab", 0), ("slab", 1)]

    def dma(q, out, in_, r, w):
        return S.add(q, lambda e: e.dma_start(out=out, in_=in_), r, w, dma=True)

    def pe(fn, r, w):
        return S.add("pe", fn, r, w)

    def act(fn, r, w):
        return S.add("act", fn, r, w)

    def dve(fn, r, w):
        return S.add("dve", fn, r, w)

    def rsq(dst, src, scale, rkeys, wkey):
        act(lambda e: e.activation(out=dst, in_=src, func=AF.Sqrt, scale=scale, bias=EPS), rkeys, [wkey])
        dve(lambda e: e.reciprocal(out=dst, in_=dst), [wkey], [wkey])

    def mm(out, lhsT, rhs, start=True, stop=True):
        return lambda e: e.matmul(out, lhsT, rhs, start=start, stop=stop)

    def mm_group(out, pairs):
        n = len(pairs)

        def fn(e):
            ins = None
            for i, (a, b) in enumerate(pairs):
                ins = e.matmul(out, a, b, start=(i == 0), stop=(i == n - 1))
            return ins
        return fn

    dve(lambda e: e.memset(RK[:, :], 0.0), [], RKALL)
    dve(lambda e: e.memset(X[:, :, :], 0.0), [], [("x", t) for t in range(GT)])
    dve(lambda e: e.memset(QT[:, :, :, :], 0.0), [], [("qt", t) for t in range(GT)])
    dma("sp", ident[:, :], c_ident[:, :], [], ["ident"])
    for j in range(2):
        dma("sp", ident2[:, j, :], c_ident[:, :], [], ["ident2"])
        dma("sp", ident2s[:, j, :], c_ident[0:64, 0:64], [], ["ident2s"])
    dma("sp", trip[:, :], c_trip[:, :], [], ["trip"])
    dma("sp", cmp_[:, :, :], c_cmp[:, :, :], [], ["cmp"])
    dma("sp", idxv, c_idx[:, :, :], RKALL, ["idxv"])
    dma("sp", rbb[:, :], rb_in[0:1, :].partition_broadcast(128), [], ["rbb"])
    dma("sp", lamt, lam_in[:, :].rearrange("l n -> (l n)").partition_broadcast(128), RKALL, ["lamt"])
    dma("sp", gtmp, g_gla[:, :].rearrange("l n -> (l n)").partition_broadcast(128), RKALL, ["gtmp"])
    dma("sp", gdf[:, :], g_dif[:, :].rearrange("l n -> (l n)").partition_broadcast(128), [], ["gdf"])
    dma("pool", w2s[:, :].rearrange("p (l n) -> p l n", l=L), w2a[:, :, :].rearrange("l p n -> p l n"), [], ["w2s"])
    dve(lambda e: e.memset(glrT[:, :], 1.0), [], ["glrT"])
    dve(lambda e: e.memset(vaug[:, :, :], 1.0), [], ["vaug"])
    dve(lambda e: e.memset(Sst[:, :, :, :], 0.0), [], ["Sst"])
    dve(lambda e: e.memset(Sb[:, :, :, :], 0.0), [], ["Sb"])
    for l in range(L):
        for h in range(4):
            dve(lambda e, l=l, h=h: e.tensor_copy(out=ggl4[:, l, h, :], in_=gtmp[:, l * 128:(l + 1) * 128]),
                ["gtmp"], ["ggl4"])

    for l in range(L):
        lam_init = 0.8 - 0.6 * math.exp(-0.3 * l)
        for j in range(2):
            a = lamt[:, l * 256 + j * 128: l * 256 + j * 128 + 64]
            b = lamt[:, l * 256 + j * 128 + 64: l * 256 + j * 128 + 128]
            dve(lambda e, a=a, b=b, j=j: e.tensor_tensor(out=tmpb[:, j * 64:(j + 1) * 64], in0=a, in1=b, op=ALU.mult),
                ["lamt"], [("tmpb", j)])
            dve(lambda e, j=j: e.tensor_reduce(out=ssq[:, j:j + 1], in_=tmpb[:, j * 64:(j + 1) * 64],
                                                axis=mybir.AxisListType.X, op=ALU.add), [("tmpb", j)], [("ssq", j)])
        act(lambda e: e.activation(out=rstd[:, 0:2], in_=ssq[:, 0:2], func=AF.Exp), [("ssq", 0), ("ssq", 1)], ["rstd01"])
        dve(lambda e, l=l: e.tensor_tensor(out=lamv[:, l:l + 1], in0=rstd[:, 0:1], in1=rstd[:, 1:2], op=ALU.subtract),
            ["rstd01"], [("lamv", l)])
        dve(lambda e, l=l, li=lam_init: e.tensor_scalar(out=nlam[:, l:l + 1], in0=lamv[:, l:l + 1], scalar1=li,
                                                        scalar2=-1.0, op0=ALU.add, op1=ALU.mult),
            [("lamv", l)], [("nlam", l)])
        dve(lambda e, l=l, li=lam_init: e.tensor_scalar(out=gdf[:, l * 128:(l + 1) * 128], in0=gdf[:, l * 128:(l + 1) * 128],
                                                        scalar1=1.0 - li, scalar2=None, op0=ALU.mult), ["gdf"], ["gdf"])

    for h in range(4):
        dve(lambda e, h=h: e.tensor_copy(out=ch[:, h:h + 1], in_=rbb[:, 15 * 4 + h:15 * 4 + h + 1]), ["rbb"], [("ch", h)])
    for h in range(4):
        for j in range(2):
            dve(lambda e, h=h, j=j: e.memset(b2[:, j, h, :], 0.0), [], [("b2h", h, j)])
            for bk in range(33):
                if bk < 32:
                    sc_ap = rbb[:, bk * 4 + h: bk * 4 + h + 1]
                    dve(lambda e, j=j, bk=bk, sc_ap=sc_ap: e.tensor_scalar(
                        out=tmpb[:, 0:128], in0=idxv[:, j, :], scalar1=float(bk), scalar2=sc_ap,
                        op0=ALU.is_equal, op1=ALU.mult), ["idxv", "rbb"], ["tmpb0"])
                else:
                    dve(lambda e, j=j: e.tensor_scalar(
                        out=tmpb[:, 0:128], in0=idxv[:, j, :], scalar1=32.0, scalar2=NEG,
                        op0=ALU.is_equal, op1=ALU.mult), ["idxv"], ["tmpb0"])
                dve(lambda e, h=h, j=j: e.tensor_tensor(out=b2[:, j, h, :], in0=b2[:, j, h, :], in1=tmpb[:, 0:128],
                                                        op=ALU.add), ["tmpb0", ("b2h", h, j)], [("b2h", h, j)])
            dve(lambda e, h=h, j=j: e.tensor_scalar(out=BMS[:, j, h, :], in0=b2[:, j, h, :], scalar1=ch[:, h:h + 1],
                                                    scalar2=None, op0=ALU.subtract), [("b2h", h, j), ("ch", h)], ["BMS"])
    SETUP_KEYS = ["idxv", "lamt", "gtmp", "tmpb0", ("tmpb", 0), ("tmpb", 1)] + [("b2h", h, j) for h in range(4) for j in range(2)]

    def norm_tile(tl, NR, extra_w):
        act(lambda e: e.activation(out=junk[0:NR, :], in_=X[0:NR, tl, :], func=AF.Square, accum_out=ssq[0:NR, 2:3]),
            [("x", tl)], ["junk", ("ssq", 2)])
        rsq(rstd[0:NR, 2:3], ssq[0:NR, 2:3], 1.0 / D, [("ssq", 2)], ("rstd", 2))
        dve(lambda e: e.scalar_tensor_tensor(out=xn[0:NR, :], in0=X[0:NR, tl, :], scalar=rstd[0:NR, 2:3], in1=GB[0:NR, :],
                                             op0=ALU.mult, op1=ALU.mult), [("x", tl), ("rstd", 2), "GB"], ["xn"])

        def tr(e):
            ins = None
            for k in range(KC):
                ins = e.transpose(PT[:, k * 128:k * 128 + NR], xn[0:NR, k * 128:(k + 1) * 128], ident[0:NR, 0:NR])
            return ins
        pe(tr, ["xn", "ident"], ["PT"])
        dve(lambda e: e.tensor_copy(out=XTg[:, :, tl * 128:tl * 128 + NR],
                                    in_=PT[:, :].rearrange("p (k t) -> p k t", k=KC)[:, :, 0:NR]),
            ["PT"], [("xtg", tl)] + extra_w)

    def tile_gla(l, sample, tl, NR, seq):
        W1 = RW[:, 0:KC * NG1].rearrange("p (k n) -> p k n", k=KC)
        xc = [XTg[:, k, tl * 128:tl * 128 + NR] for k in range(KC)]
        xk = ("xtg", tl)
        if sample:
            dma("sp", S0[:, :, :], sg[l, seq, :, :].rearrange("(h p) v -> p h v", p=64), [], ["S0"])
            dve(lambda e: e.tensor_copy(out=S0b[:, :, :], in_=S0[:, :, :]), ["S0"], ["S0b"])
        p0 = nextps()
        pe(mm_group(BANK[p0][0:16, 0:NR], [(W1[:, k, C_GLR:C_GLR + 16], xc[k]) for k in range(KC)]), [xk, "RW"], [("bank", p0)])
        dve(lambda e: e.tensor_copy(out=glrT[0:16, 0:NR], in_=BANK[p0][0:16, 0:NR]), [("bank", p0)], ["glrT"])
        p1 = nextps()
        pe(mm(BANK[p1][0:NR, 0:256], glrT[:, 0:NR], w2s[:, l * 256:(l + 1) * 256]), ["glrT", "w2s"], [("bank", p1)])
        act(lambda e: e.activation(out=lsp[0:NR, :], in_=BANK[p1][0:NR, 0:256], func=AF.Exp, scale=-1.0),
            [("bank", p1)], ["lsp"])
        act(lambda e: e.activation(out=lsp[0:NR, :], in_=lsp[0:NR, :], func=AF.Ln, bias=1.0), ["lsp"], ["lsp"])
        _stop(1.2)
        p2 = nextps()
        pe(mm(BANK[p2][0:NR, 0:256], trip[0:NR, 0:NR], lsp[0:NR, :]), ["lsp", "trip"], [("bank", p2)])
        act(lambda e: e.activation(out=enb[0:NR, :], in_=BANK[p2][0:NR, 0:256], func=AF.Exp, scale=-1.0),
            [("bank", p2)], ["enb"])
        p3 = nextps()

        def bt(e):
            ins = None
            for h in range(4):
                ins = e.matmul(BANK[p3][0:64, h * 128:h * 128 + NR], lsp[0:NR, h * 64:(h + 1) * 64], trip[0:NR, 0:NR],
                               start=True, stop=True)
            return ins
        pe(bt, ["lsp", "trip"], [("bank", p3)])
        p3v = BANK[p3][0:64, :].rearrange("p (h t) -> p h t", h=4)[:, :, 0:NR]
        act(lambda e: e.activation(out=ebT[:, :, 0:NR], in_=p3v, func=AF.Exp), [("bank", p3)], ["ebT"])
        act(lambda e: e.activation(out=enbT[:, :, 0:NR], in_=p3v, func=AF.Exp, scale=-1.0), [("bank", p3)], ["enbT"])
        _stop(1.3)
        p4 = nextps()
        pe(mm_group(BANK[p4][0:NR, 0:256], [(xc[k], W1[:, k, C_GK:C_GK + 256]) for k in range(KC)]), [xk, "RW"],
           [("bank", p4)])
        dve(lambda e: e.tensor_tensor(out=ktok[0:NR, :], in0=BANK[p4][0:NR, 0:256], in1=enb[0:NR, :], op=ALU.mult),
            [("bank", p4), "enb"], ["ktok"])
        p5 = nextps()
        pe(mm_group(BANK[p5][0:NR, :], [(xc[k], W1[:, k, C_GV:C_GV + 512]) for k in range(KC)]), [xk, "RW"], [("bank", p5)])
        act(lambda e: e.activation(out=vtok[0:NR, :], in_=BANK[p5][0:NR, :], func=AF.Copy), [("bank", p5)], ["vtok"])
        _stop(1.4)
        p6 = nextps()
        p7 = nextps()

        def kq(e):
            ins = None
            for pb, c0 in ((p6, C_GK), (p7, C_GQ)):
                for h in range(4):
                    for k in range(KC):
                        ins = e.matmul(BANK[pb][0:64, h * 128:h * 128 + NR], W1[:, k, c0 + h * 64:c0 + (h + 1) * 64], xc[k],
                                       start=(k == 0), stop=(k == KC - 1))
            return ins
        pe(kq, [xk, "RW"], [("bank", p6), ("bank", p7)])
        p6v = BANK[p6][0:64, :].rearrange("p (h t) -> p h t", h=4)[:, :, 0:NR]
        p7v = BANK[p7][0:64, :].rearrange("p (h t) -> p h t", h=4)[:, :, 0:NR]
        dve(lambda e: e.tensor_tensor(out=ktT[:, :, 0:NR], in0=p6v, in1=enbT[:, :, 0:NR], op=ALU.mult),
            [("bank", p6), "enbT"], ["ktT"])
        dve(lambda e: e.scalar_tensor_tensor(out=qtT[:, :, 0:NR], in0=p7v, scalar=0.125, in1=ebT[:, :, 0:NR],
                                             op0=ALU.mult, op1=ALU.mult), [("bank", p7), "ebT"], ["qtT"])
        _stop(1.5)
        p8 = nextps()

        def sc(e):
            ins = None
            for h in range(4):
                ins = e.matmul(BANK[p8][0:NR, h * 128:h * 128 + NR], ktT[:, h, 0:NR], qtT[:, h, 0:NR], start=True, stop=True)
            return ins
        pe(sc, ["ktT", "qtT"], [("bank", p8)])
        p8v = BANK[p8][0:NR, :].rearrange("p (h t) -> p h t", h=4)[:, :, 0:NR]
        dve(lambda e: e.tensor_tensor(out=scT[0:NR, :, 0:NR], in0=p8v, in1=cmp_[0:NR, :, 0:NR], op=ALU.mult),
            [("bank", p8), "cmp"], ["scT"])
        _stop(1.6)
        p9 = nextps()

        def oo(e):
            ins = None
            for h in range(4):
                out = BANK[p9][0:NR, h * 128:(h + 1) * 128]
                e.matmul(out, scT[0:NR, h, 0:NR], vtok[0:NR, h * 128:(h + 1) * 128], start=True, stop=False)
                sbh = S0b[:, h, :] if sample else Sb[:, l, h, :]
                ins = e.matmul(out, qtT[:, h, 0:NR], sbh, start=False, stop=True)
            return ins
        pe(oo, ["scT", "vtok", "qtT", "S0b", "Sb"], [("bank", p9)])
        _stop(1.7)
        pu = nextps()

        def uu(e):
            ins = None
            for h in range(4):
                ins = e.matmul(BANK[pu][0:64, h * 128:(h + 1) * 128], ktok[0:NR, h * 64:(h + 1) * 64],
                               vtok[0:NR, h * 128:(h + 1) * 128], start=True, stop=True)
            return ins
        pe(uu, ["ktok", "vtok"], [("bank", pu)])
        for h in range(4):
            dcol = ebT[:, h, NR - 1:NR]
            dve(lambda e, h=h, dcol=dcol: e.tensor_scalar(out=udt[:, h, :], in0=BANK[pu][0:64, h * 128:(h + 1) * 128],
                                                          scalar1=dcol, scalar2=None, op0=ALU.mult),
                [("bank", pu), "ebT"], [("udt", h)])
            if sample:
                dve(lambda e, h=h, dcol=dcol: e.scalar_tensor_tensor(out=udt[:, h, :], in0=S0[:, h, :], scalar=dcol,
                                                                     in1=udt[:, h, :], op0=ALU.mult, op1=ALU.add),
                    ["S0", "ebT", ("udt", h)], [("udt", h)])
            else:
                dve(lambda e, h=h, dcol=dcol: e.scalar_tensor_tensor(out=Sst[:, l, h, :], in0=Sst[:, l, h, :], scalar=dcol,
                                                                     in1=udt[:, h, :], op0=ALU.mult, op1=ALU.add),
                    ["Sst", "ebT", ("udt", h)], ["Sst"])
        if sample:
            dma("sp", ngs_o[l, seq, :, :].rearrange("(h p) v -> p h v", p=64), udt[:, :, :],
                [("udt", h) for h in range(4)], [])
        else:
            dve(lambda e: e.tensor_copy(out=Sb[:, l, :, :], in_=Sst[:, l, :, :]), ["Sst"], ["Sb"])
        _stop(1.8)
        pg = nextps()
        pe(mm_group(BANK[pg][0:NR, :], [(xc[k], W1[:, k, C_GG:C_GG + 512]) for k in range(KC)]), [xk, "RW"], [("bank", pg)])
        act(lambda e: e.activation(out=gsl[0:NR, :], in_=BANK[pg][0:NR, :], func=AF.Silu), [("bank", pg)], ["gsl"])
        dve(lambda e: e.tensor_tensor(out=gsl[0:NR, :], in0=gsl[0:NR, :],
                                      in1=ggl4[0:NR, l, :, :].rearrange("p h v -> p (h v)"), op=ALU.mult),
            ["gsl", "ggl4"], ["gsl"])
        for h in range(4):
            act(lambda e, h=h: e.activation(out=junk[0:NR, h * 128:(h + 1) * 128], in_=BANK[p9][0:NR, h * 128:(h + 1) * 128],
                                            func=AF.Square, accum_out=ssq[0:NR, 4 + h:5 + h]),
                [("bank", p9)], ["junk", ("ssq", 4 + h)])
        rsq(rstd[0:NR, 4:8], ssq[0:NR, 4:8], 1.0 / 128, [("ssq", 4 + h) for h in range(4)], "rstd4")
        for h in range(4):
            dve(lambda e, h=h: e.scalar_tensor_tensor(out=mixb[0:NR, h * 128:(h + 1) * 128],
                                                      in0=BANK[p9][0:NR, h * 128:(h + 1) * 128],
                                                      scalar=rstd[0:NR, 4 + h:5 + h], in1=gsl[0:NR, h * 128:(h + 1) * 128],
                                                      op0=ALU.mult, op1=ALU.mult),
                [("bank", p9), "rstd4", "gsl"], ["mixb"])

        def tr(e):
            ins = None
            for h in range(4):
                ins = e.transpose(PT[:, h * 128:h * 128 + NR], mixb[0:NR, h * 128:(h + 1) * 128], ident[0:NR, 0:NR])
            return ins
        pe(tr, ["mixb", "ident"], ["PT"])
        dve(lambda e: e.tensor_copy(out=MT[:, 0:4, tl * 128:tl * 128 + NR],
                                    in_=PT[:, 0:512].rearrange("p (h t) -> p h t", h=4)[:, :, 0:NR]), ["PT"], [("mt", tl)])
        _stop(1.9)

    def tile_diff(l, sample, tl, NR, row0, gt):
        W2 = RW[:, 0:KC * NG2].rearrange("p (k n) -> p k n", k=KC)
        xc = [XTg[:, k, tl * 128:tl * 128 + NR] for k in range(KC)]
        xk = ("xtg", tl)
        pq = nextps()
        pk = nextps()

        def dqk(e):
            ins = None
            for pb, c0 in ((pq, 0), (pk, 512)):
                for h in range(4):
                    for k in range(KC):
                        ins = e.matmul(BANK[pb][:, h * 128:h * 128 + NR], W2[:, k, c0 + h * 128:c0 + (h + 1) * 128], xc[k],
                                       start=(k == 0), stop=(k == KC - 1))
            return ins
        pe(dqk, [xk, "RW"], [("bank", pq), ("bank", pk)])
        pqv = BANK[pq][:, :].rearrange("p (h t) -> p h t", h=4)
        for m in range(2):
            rows = slice(m * 64, m * 64 + 64)
            act(lambda e, m=m, rows=rows: e.activation(out=QT[rows, :, m, tl * 128:tl * 128 + NR], in_=pqv[rows, :, 0:NR],
                                                       func=AF.Copy, scale=0.125), [("bank", pq)], [("qt", tl)])
        act(lambda e: e.activation(out=kts[:, :, 0:NR], in_=BANK[pk][:, :].rearrange("p (h t) -> p h t", h=4)[:, :, 0:NR],
                                   func=AF.Copy), [("bank", pk)], ["kts"])
        _stop(2.1)
        if not sample:
            dma("sp", KTd[l][:, :, gt * 128:(gt + 1) * 128].rearrange("h d t -> d h t"), kts[:, :, :], ["kts"], [("ktd", l)])
        else:
            dve(lambda e: e.tensor_copy(out=ktsS[:, tl, :, :], in_=kts[:, :, 0:64]), ["kts"], [("ktsa", tl)])
        _stop(2.2)
        for which, coff, dst in ((0, 512, nk_o), (1, 1024, nv_o)):
            pp = nextps()
            pe(mm_group(BANK[pp][0:NR, :], [(xc[k], W2[:, k, coff:coff + 512]) for k in range(KC)]), [xk, "RW"],
               [("bank", pp)])
            dve(lambda e, pp=pp, which=which: e.tensor_copy(out=ev32[which][0:NR, :], in_=BANK[pp][0:NR, :]),
                [("bank", pp)], [("ev", which)])
            dma("sp", dst[l, row0:row0 + NR, :], ev32[which][0:NR, :], [("ev", which)], [])
            _stop(2.3)
            if which == 1:
                act(lambda e: e.activation(out=vaug[0:NR, :, 0:128],
                                           in_=ev32[1][0:NR, :].rearrange("p (h v) -> p h v", h=4), func=AF.Copy),
                    [("ev", 1)], ["vaug"])
                _stop(2.35)
                if not sample:
                    dma("sp", Vd[l][:, :, gt, :].rearrange("h t c -> t h c"), vaug[:, :, :], ["vaug"], [("vd", l)])
                else:
                    dve(lambda e: e.tensor_copy(out=vaugS[:, tl, :, :], in_=vaug[0:64, :, :]), ["vaug"], [("vauga", tl)])
                _stop(2.4)

    chunk_i = [0]

    def attend(l, tl, h, NQ, blocks, rkeys, a0, started, last):
        qc = slice(tl * 128, tl * 128 + NQ)
        CW = 2 * NQ
        CB = 1024 // CW
        I2 = (ident2[:, :, :] if NQ == 128 else ident2s[:, :, :]).rearrange("p a q -> p (a q)")
        chunks = []
        cur = []
        for bi, blk in enumerate(blocks):
            if cur and (len(cur) == CB or blk[2] != cur[0][1][2]):
                chunks.append(cur)
                cur = []
            cur.append((bi, blk))
        if cur:
            chunks.append(cur)
        nbt = len(blocks)
        for chn in chunks:
            c = chunk_i[0] % 2
            chunk_i[0] += 1
            base = c * 1024
            nk = chn[0][1][2]
            bk = [("bank", 2 * c), ("bank", 2 * c + 1)]

            def qk(e, chn=chn, base=base):
                ins = None
                for ci, (bi, (lhsT, v, nk_, bias)) in enumerate(chn):
                    o0 = base + ci * CW
                    if bias is not None:
                        e.matmul(PSL[0:nk_, o0:o0 + CW], bias, I2, start=True, stop=False)
                    for m in range(2):
                        ins = e.matmul(PSL[0:nk_, o0 + m * NQ:o0 + (m + 1) * NQ], lhsT, QT[:, h, m, qc],
                                       start=(bias is None), stop=True)
                return ins
            pe(qk, rkeys + [("qt", tl), "BMS", "ident2", "ident2s"], bk)
            n = len(chn)
            act(lambda e, c=c, base=base, n=n, nk=nk: e.activation(
                out=eT[c][0:nk, 0:n * CW], in_=PSL[0:nk, base:base + n * CW], func=AF.Exp, bias=ch[0:nk, h:h + 1]),
                bk + [("ch", h)], [("eT", c)])

            flags = []
            for ci, (bi, blk_) in enumerate(chn):
                for m in range(2):
                    ap_, bnk = acc_ap(a0 + m, NQ)
                    flags.append(bnk not in started)
                    started.add(bnk)

            def av(e, chn=chn, c=c, flags=flags):
                ins = None
                fi = 0
                for ci, (bi, (lhsT, v, nk_, bias)) in enumerate(chn):
                    for m in range(2):
                        ap_, bnk = acc_ap(a0 + m, NQ)
                        ins = e.matmul(ap_, eT[c][0:nk_, ci * CW + m * NQ:ci * CW + (m + 1) * NQ], v,
                                       start=flags[fi], stop=(last and bi == nbt - 1), skip_group_check=True)
                        fi += 1
                return ins
            pe(av, [("eT", c)] + rkeys, ACCKEYS)
        if last:
            finalize(l, tl, h, NQ, a0)

    def finalize(l, tl, h, NQ, a0):
        R = slice(0, NQ)
        for m in range(2):
            ap_, bnk = acc_ap(a0 + m, NQ)
            dve(lambda e, m=m, ap_=ap_: e.tensor_copy(out=osb[R, m, :], in_=ap_), ACCKEYS, [("osb", m)])
        dve(lambda e: e.reciprocal(out=rs[R, 0:2], in_=osb[R, :, 128]), [("osb", 0), ("osb", 1)], ["rs"])
        dve(lambda e: e.tensor_tensor(out=rs[R, 1:2], in0=rs[R, 1:2], in1=nlam[R, l:l + 1], op=ALU.mult),
            ["rs", ("nlam", l)], ["rs"])
        dve(lambda e: e.tensor_scalar(out=od[R, :], in0=osb[R, 0, 0:128], scalar1=rs[R, 0:1], scalar2=None,
                                      op0=ALU.mult), [("osb", 0), "rs"], ["od"])
        dve(lambda e: e.scalar_tensor_tensor(out=od[R, :], in0=osb[R, 1, 0:128], scalar=rs[R, 1:2],
                                             in1=od[R, :], op0=ALU.mult, op1=ALU.add), [("osb", 1), "rs", "od"], ["od"])
        act(lambda e: e.activation(out=junk[R, 0:128], in_=od[R, :], func=AF.Square, accum_out=ssq[R, 3:4]),
            ["od"], ["junk", ("ssq", 3)])
        rsq(rstd[R, 3:4], ssq[R, 3:4], 1.0 / 128, [("ssq", 3)], ("rstd", 3))
        dve(lambda e: e.scalar_tensor_tensor(out=odb[R, :], in0=od[R, :], scalar=rstd[R, 3:4],
                                             in1=gdf[R, l * 128:(l + 1) * 128], op0=ALU.mult, op1=ALU.mult),
            ["od", ("rstd", 3), "gdf"], ["odb"])
        pe(lambda e: e.transpose(PT[:, 0:NQ], odb[R, :], ident[R, R]), ["odb", "ident"], ["PT"])
        dve(lambda e: e.tensor_copy(out=MT[:, 4 + h, tl * 128:tl * 128 + NQ], in_=PT[:, 0:NQ]), ["PT"], [("mt", tl)])

    def _main(groups):
        first_sample = [True]
        for kind, g in groups:
            sample = (kind == "s")
            NR = 64 if sample else 128
            if sample:
                r0 = SEQ + g * GT * 64
                dma("sp", X[0:64, :, :], xin[r0:r0 + GT * 64, :].rearrange("(t p) d -> p t d", p=64),
                    [], [("x", t) for t in range(GT)])
            else:
                r0 = g * GT * 128
                dma("sp", X[:, :, :], xin[r0:r0 + GT * 128, :].rearrange("(t p) d -> p t d", p=128),
                    [], [("x", t) for t in range(GT)])
            for l in range(L):
                dma("sp", GB[:, :], g_mix[l:l + 1, :].partition_broadcast(128), [], ["GB"])
                _stop(1)
                for tl in range(GT):
                    norm_tile(tl, NR, SETUP_KEYS if (g == 0 and l == 0 and not sample) else [])
                _stop(1.1)
                dma("pool", RW[:, 0:KC * NG1].rearrange("p (k n) -> p k n", k=KC),
                    w_in[l, :, 0:NG1].rearrange("(k p) n -> p k n", p=128), [], RWALL)
                for tl in range(GT):
                    tile_gla(l, sample, tl, NR, g * GT + tl)
                _stop(2)
                dma("pool", RW[:, 0:KC * NG2].rearrange("p (k n) -> p k n", k=KC),
                    w_in[l, :, NG1:DIN].rearrange("(k p) n -> p k n", p=128), [], RWALL)
                for tl in range(GT):
                    tile_diff(l, sample, tl, NR, r0 + tl * NR, g * GT + tl)
                _stop(2.5)
                if not sample:
                    gt0 = g * GT
                    NKB = gt0 + GT
                    KTh = RK[:, 0:NKB * 128]
                    Vh = RK[:, NPT * 128:NPT * 128 + NKB * VW].rearrange("p (j c) -> p j c", c=VW)
                    for h in range(4):
                        dma("sp", KTh, KTd[l][h, :, 0:NKB * 128], [("ktd", l)], RKALL)
                        dma("sp", Vh, Vd[l][h, :, 0:NKB, :],
                            [("vd", l)], RKALL)
                        started = set()
                        nfar = max(0, gt0 - 1)
                        for j in range(nfar):
                            c = chunk_i[0] % 2
                            chunk_i[0] += 1
                            base = c * 1024
                            bk = [("bank", 2 * c), ("bank", 2 * c + 1)]

                            def qkf(e, j=j, base=base, h=h, KTh=KTh):
                                ins = None
                                for m in range(2):
                                    ins = e.matmul(PSL[:, base + m * 512:base + (m + 1) * 512], KTh[:, j * 128:(j + 1) * 128],
                                                   QT[:, h, m, :], start=True, stop=True)
                                return ins
                            pe(qkf, ["RK"] + [("qt", t) for t in range(GT)], bk)
                            act(lambda e, c=c, base=base, h=h: e.activation(
                                out=eT[c][:, :], in_=PSL[:, base:base + 1024], func=AF.Exp, bias=ch[:, h:h + 1]),
                                bk + [("ch", h)], [("eT", c)])
                            flags = []
                            for tl in range(GT):
                                for m in range(2):
                                    ap_, bnk = acc_ap(tl * 2 + m, 128)
                                    flags.append(bnk not in started)
                                    started.add(bnk)

                            def avf(e, j=j, c=c, flags=flags, Vh=Vh):
                                ins = None
                                fi = 0
                                for tl in range(GT):
                                    for m in range(2):
                                        ap_, bnk = acc_ap(tl * 2 + m, 128)
                                        ins = e.matmul(ap_, eT[c][:, m * 512 + tl * 128:m * 512 + (tl + 1) * 128], Vh[:, j, :],
                                                       start=flags[fi], stop=False, skip_group_check=True)
                                        fi += 1
                                return ins
                            pe(avf, [("eT", c), "RK"], ACCKEYS)
                        for tl in range(GT):
                            i = gt0 + tl
                            blocks = []
                            for j in range(nfar, i + 1):
                                bias = None
                                if j == i:
                                    bias = BMS[:, 0, h, :]
                                elif j == i - 1:
                                    bias = BMS[:, 1, h, :]
                                blocks.append((KTh[:, j * 128:(j + 1) * 128], Vh[:, j, :], 128, bias))
                            attend(l, tl, h, 128, blocks, ["RK"], tl * 2, started, True)
                else:
                    for tl in range(GT):
                        seq = g * GT + tl
                        for h in range(4):
                            dma("pool", ckb, ck[l, seq, h, :, :], [], ["ckb"] + (RKALL if first_sample[0] else []))
                            dma("pool", cvb[:, :, 0:128], cv[l, seq, h, :, :].rearrange("t (j v) -> t j v", v=128), [],
                                ["cvb"] + (RKALL if first_sample[0] else []))
                            if first_sample[0]:
                                dve(lambda e: e.memset(cvb[:, :, 128:VW], 1.0), [], ["cvb"])
                            first_sample[0] = False
                            blocks = []
                            for j in range(PB):
                                bias = BMS[0:64, 1, h, :] if j == PB - 1 else None
                                blocks.append((ckb[:, j * 128:(j + 1) * 128], cvb[:, j, :], 128, bias))
                            blocks.append((ktsS[:, tl, h, :], vaugS[:, tl, h, :], 64, BMS[0:64, 0, h, 0:64]))
                            attend(l, tl, h, 64, blocks, ["ckb", "cvb", ("ktsa", tl), ("vauga", tl)], 0, set(), True)
                _stop(3)
                if DEBUG and l == 0 and g == 0:
                    dma("sp", dbg_o[1 if sample else 0, :, :], MT[:, :, :].rearrange("p c t -> p (c t)"),
                        [("mt", t) for t in range(GT)], [])
                WOv = RW[:, 0:8 * D].rearrange("p (k n) -> p k n", k=8)
                dma("pool", WOv, w_out[l, :, :].rearrange("(k p) n -> p k n", p=128), [], RWALL)
                for tl in range(GT):
                    for half in range(2):
                        pa = nextps()
                        pe(mm_group(BANK[pa][0:NR, :], [(MT[:, cch, tl * 128:tl * 128 + NR], WOv[:, cch, half * 512:(half + 1) * 512])
                                                        for cch in range(8)]), [("mt", tl), "RW"], [("bank", pa)])
                        if DEBUG and l == 0 and g == 0 and not sample and tl == 0 and half == 0:
                            dve(lambda e, pa=pa: e.tensor_copy(out=ev32[0][:, :], in_=BANK[pa][:, :]), [("bank", pa)], [("ev", 0)])
                            dma("sp", dbgx_o[2, :, 0, 0:512], ev32[0][:, :], [("ev", 0)], [])
                        dve(lambda e, pa=pa, half=half, tl=tl, NR=NR: e.tensor_tensor(
                            out=X[0:NR, tl, half * 512:(half + 1) * 512], in0=BANK[pa][0:NR, :],
                            in1=X[0:NR, tl, half * 512:(half + 1) * 512], op=ALU.add), [("bank", pa), ("x", tl)], [("x", tl)])
                _stop(4)
                if DEBUG and l == 0 and g == 0 and not sample:
                    dma("sp", dbgx_o[0, :, :, :], X[:, :, :], [("x", t) for t in range(GT)], [])
                dma("sp", GB[:, :], g_ffn[l:l + 1, :].partition_broadcast(128), [], ["GB"])
                for tl in range(GT):
                    norm_tile(tl, NR, ["RK", "ckb", "cvb"])
                W2h = RW[:, 0:NFF * 512].rearrange("p (j n) -> p j n", j=NFF)
                NTK = GT * 128
                for j in range(NFF):
                    sl = j % 2
                    so = NFF * 512 + sl * KC * 256
                    slab = RW[:, so:so + KC * 256].rearrange("p (k n) -> p k n", k=KC)
                    dma("pool", slab[:, :, 0:128], w_f1[l, :, j * 128:(j + 1) * 128].rearrange("(k p) n -> p k n", p=128),
                        [], ["RW", ("slab", sl)])
                    dma("pool", slab[:, :, 128:256],
                        w_f1[l, :, DFF + j * 128:DFF + (j + 1) * 128].rearrange("(k p) n -> p k n", p=128), [],
                        ["RW", ("slab", sl)])
                    pg = nextps()
                    pu = nextps()
                    xr = [("xtg", t) for t in range(GT)]
                    pe(mm_group(BANK[pg][:, 0:NTK], [(slab[:, k, 0:128], XTg[:, k, :]) for k in range(KC)]),
                       [("slab", sl)] + xr, [("bank", pg)])
                    pe(mm_group(BANK[pu][:, 0:NTK], [(slab[:, k, 128:256], XTg[:, k, :]) for k in range(KC)]),
                       [("slab", sl)] + xr, [("bank", pu)])
                    act(lambda e, pg=pg: e.activation(out=gsl[:, 0:NTK], in_=BANK[pg][:, 0:NTK], func=AF.Silu),
                        [("bank", pg)], ["gsl"])
                    dve(lambda e, j=j, pu=pu: e.tensor_tensor(out=ACTB[:, j, :], in0=gsl[:, 0:NTK], in1=BANK[pu][:, 0:NTK],
                                                              op=ALU.mult), ["gsl", ("bank", pu)], ["actb"])
                for half in range(2):
                    dma("pool", W2h, w_f2[l, :, half * 512:(half + 1) * 512].rearrange("(j p) n -> p j n", p=128),
                        [], ["RW", "w2h"])
                    for tl in range(GT):
                        pa = nextps()
                        pe(mm_group(BANK[pa][0:NR, :], [(ACTB[:, j, tl * 128:tl * 128 + NR], W2h[:, j, :]) for j in range(NFF)]),
                           ["actb", "w2h"], [("bank", pa)])
                        dve(lambda e, pa=pa, half=half, tl=tl, NR=NR: e.tensor_tensor(
                            out=X[0:NR, tl, half * 512:(half + 1) * 512], in0=BANK[pa][0:NR, :],
                            in1=X[0:NR, tl, half * 512:(half + 1) * 512], op=ALU.add), [("bank", pa), ("x", tl)], [("x", tl)])
                _stop(5)
                if DEBUG and l == 0 and g == 0 and not sample:
                    dma("sp", dbgx_o[1, :, :, :], X[:, :, :], [("x", t) for t in range(GT)], [])
            dma("sp", GB[:, :], g_fin[0:1, :].partition_broadcast(128), [], ["GB"])
            for tl in range(GT):
                act(lambda e, tl=tl, NR=NR: e.activation(out=junk[0:NR, :], in_=X[0:NR, tl, :], func=AF.Square,
                                                  accum_out=ssq[0:NR, 2:3]), [("x", tl)], ["junk", ("ssq", 2)])
                rsq(rstd[0:NR, 2:3], ssq[0:NR, 2:3], 1.0 / D, [("ssq", 2)], ("rstd", 2))
                dve(lambda e, tl=tl, NR=NR: e.scalar_tensor_tensor(out=junk[0:NR, :], in0=X[0:NR, tl, :], scalar=rstd[0:NR, 2:3],
                                                            in1=GB[0:NR, :], op0=ALU.mult, op1=ALU.mult),
                    [("x", tl), ("rstd", 2), "GB"], ["junk"])
                dma("sp", y_o[r0 + tl * NR:r0 + (tl + 1) * NR, :], junk[0:NR, :], ["junk"], [])
            if kind == "p" and g == NPT // GT - 1:
                for l in range(L):
                    dma("sp", ngp_o[l, :, :].rearrange("(h p) v -> p h v", p=64), Sst[:, l, :, :], ["Sst"], [])

    groups = [("p", g) for g in range(NPT // GT)] + [("s", g) for g in range(NSEQ // GT)]
    try:
        _stop(0)
        _main(groups)
    except StopBuild:
        pass
    S.emit(nc, st)
    st.close()
    return nc


def _run(inp, SEQ, PAST, L):
    NPT = SEQ // 128
    PB = PAST // 128
    nc = build(NPT, PB, L)
    f32 = np.float32
    xin = np.concatenate([np.asarray(inp["x_prompt"], f32).reshape(SEQ, D),
                          np.asarray(inp["x_sample"], f32).reshape(NSEQ * 64, D)], axis=0)
    ckh = np.ascontiguousarray(np.transpose(np.asarray(inp["cache_diff_k"], f32), (0, 1, 3, 4, 2)))
    cvv = np.asarray(inp["cache_diff_v"], f32).reshape(L, NSEQ, PB, 128, 4, 128)
    cvh = np.ascontiguousarray(np.transpose(cvv, (0, 1, 4, 3, 2, 5))).reshape(L, NSEQ, 4, 128, PB * 128)
    w2a = np.concatenate([np.asarray(inp["gla_w_alpha2"], f32), np.asarray(inp["gla_b_alpha"], f32)[:, None, :]], axis=1)
    m = dict(
        xin=xin, w_in=np.asarray(inp["w_in"], f32), w_out=np.asarray(inp["w_out"], f32),
        w_f1=np.asarray(inp["w_ffn_in"], f32), w_f2=np.asarray(inp["w_ffn_out"], f32), w2a=np.ascontiguousarray(w2a),
        g_mix=np.asarray(inp["norm_mix_g"], f32), g_ffn=np.asarray(inp["norm_ffn_g"], f32),
        g_fin=np.asarray(inp["final_norm_g"], f32).reshape(1, D), g_gla=np.asarray(inp["gla_norm_g"], f32),
        g_dif=np.asarray(inp["diff_norm_g"], f32), lam_in=np.asarray(inp["diff_lambda"], f32).reshape(L, 256),
        rb_in=np.asarray(inp["rel_bias"], f32).reshape(1, 128), ck=ckh, cv=cvh,
        sg=np.asarray(inp["state_gla"], f32).reshape(L, NSEQ, 256, 128))
    hc = _host_consts(0)
    m["c_idx"], m["c_ident"], m["c_trip"], m["c_cmp"] = hc["c_idx"], hc["c_ident"], hc["c_trip"], hc["c_cmp"]
    res = run_bass_kernel_spmd(nc, [m], core_ids=[0])
    r = res.results[0]
    global LAST_DBG
    LAST_DBG = r.get("dbg_o")
    global LAST_DBGX
    LAST_DBGX = r.get("dbgx_o")
    y = r["y_o"]
    nk = r["nk_o"]
    nv = r["nv_o"]
    return (y[:SEQ].reshape(1, SEQ, D), y[SEQ:].reshape(NSEQ, 64, D),
            nk[:, :SEQ].reshape(L, 1, SEQ, 4, 128), nv[:, :SEQ].reshape(L, 1, SEQ, 4, 128),
            r["ngp_o"].reshape(L, 1, 4, 64, 128),
            nk[:, SEQ:].reshape(L, NSEQ, 64, 4, 128), nv[:, SEQ:].reshape(L, NSEQ, 64, 4, 128),
            r["ngs_o"].reshape(L, NSEQ, 4, 64, 128))


def kernel(**inputs):
    return _run(inputs, 16384, 4096, 4)
```

```python
import math
from contextlib import ExitStack

import numpy as np
import ml_dtypes

import concourse.bass as bass
import concourse.mybir as mybir
from concourse.bass_utils import run_bass_kernel_spmd

F32 = mybir.dt.float32
BF16 = mybir.dt.bfloat16
AF = mybir.ActivationFunctionType
ALU = mybir.AluOpType

NCORE = 8
D = 1024
KC = 8
DIN = 3088
DFF = 2816
NFF = 22
NEG = -1.0e30
EPS = 1e-6
C_GQ, C_GK, C_GV, C_GG, C_GLR, C_DQ, C_DK, C_DV = 0, 256, 512, 1024, 1536, 1552, 2064, 2576


class Op:
    __slots__ = ("eng", "fn", "deps", "dma", "needed", "sem", "val", "coll")

    def __init__(self, eng, fn, dma, coll=False):
        self.eng, self.fn, self.dma, self.coll = eng, fn, dma, coll
        self.deps = []
        self.needed = False
        self.sem = None
        self.val = 0


class Sched:
    ENGS = ("pe", "act", "dve", "pool", "sp")

    def __init__(self, n_dma_sems=20):
        self.ops = {e: [] for e in self.ENGS}
        self.lastw = {}
        self.readers = {}
        self.n_dma_sems = n_dma_sems
        self.dma_hist = {e: [] for e in self.ENGS}
        self.all_dma = []

    def add(self, eng, fn, reads=(), writes=(), dma=False, coll=False):
        op = Op(eng, fn, dma, coll)
        deps = {}
        for k in reads:
            w = self.lastw.get(k)
            if w is not None:
                deps[id(w)] = w
        for k in writes:
            w = self.lastw.get(k)
            if w is not None:
                deps[id(w)] = w
            for r in self.readers.get(k, ()):
                deps[id(r)] = r
        if dma and not coll:
            h = self.dma_hist[eng]
            if len(h) >= self.n_dma_sems:
                p = h[-self.n_dma_sems]
                deps[id(p)] = p
            h.append(op)
        for d in deps.values():
            if d is op:
                continue
            if d.eng == "pe" and eng == "pe" and not d.dma and not dma:
                continue
            op.deps.append(d)
            d.needed = True
        for k in reads:
            self.readers.setdefault(k, []).append(op)
        for k in writes:
            self.lastw[k] = op
            self.readers[k] = []
        self.ops[eng].append(op)
        if dma:
            self.all_dma.append(op)
        return op

    def emit(self, nc, stack):
        esem = {e: stack.enter_context(nc.semaphore("es_" + e)) for e in self.ENGS}
        dsem = {e: [stack.enter_context(nc.semaphore("ds_%s_%d" % (e, i))) for i in range(self.n_dma_sems)]
                for e in ("sp", "pool")}
        ncoll = sum(1 for o in self.all_dma if o.coll)
        csem = [stack.enter_context(nc.semaphore("cs_%d" % i)) for i in range(ncoll)]
        fin = Op("sp", None, False)
        fin.deps = list(self.all_dma)
        for d in fin.deps:
            d.needed = True
        self.ops["sp"].append(fin)
        ci = 0
        for e in self.ENGS:
            cnt = 0
            dcnt = [0] * self.n_dma_sems
            di = 0
            for op in self.ops[e]:
                if op.coll:
                    op.sem, op.val = csem[ci], 1
                    ci += 1
                elif op.dma:
                    op.sem = dsem[e][di]
                    dcnt[di] += 16
                    op.val = dcnt[di]
                    di = (di + 1) % self.n_dma_sems
                elif op.needed:
                    cnt += 1
                    op.sem, op.val = esem[e], cnt
        handles = {"pe": nc.tensor, "act": nc.scalar, "dve": nc.vector, "pool": nc.gpsimd, "sp": nc.sync}

        def run(e):
            h = handles[e]
            seen = {}
            for op in self.ops[e]:
                for d in op.deps:
                    key = id(d.sem)
                    if seen.get(key, 0) < d.val:
                        h.wait_ge(d.sem, d.val)
                        seen[key] = d.val
                if op.fn is None:
                    continue
                ins = op.fn(h)
                if op.coll:
                    ins.then_inc(op.sem)
                elif op.dma:
                    ins.then_inc(op.sem, 16)
                elif op.needed:
                    ins.then_inc(op.sem, 1)

        with nc.Block() as block:
            @block.tensor
            def _(_e):
                run("pe")

            @block.scalar
            def _(_e):
                run("act")

            @block.vector
            def _(_e):
                run("dve")

            @block.gpsimd
            def _(_e):
                run("pool")

            @block.sync
            def _(_e):
                run("sp")


def _t5_bucket(rel):
    rel = np.asarray(rel, np.int64)
    nb, max_exact = 16, 8
    ret = np.where(rel > 0, nb, 0)
    n = np.abs(rel)
    nf = np.maximum(n, 1).astype(np.float32)
    v = (np.log(nf / np.float32(max_exact)) / np.float32(math.log(128 / max_exact))).astype(np.float32)
    large = max_exact + (v * np.float32(nb - max_exact)).astype(np.int32)
    large = np.minimum(large, nb - 1)
    return ret + np.where(n < max_exact, n, large)


def _host_consts(core):
    q = np.arange(128)[:, None]
    k = np.arange(128)[None, :]
    idx = np.zeros((128, 2, 128), np.float32)
    bd = _t5_bucket(k - q).astype(np.float32)
    allowed = (k // 64) <= (q // 64)
    idx[:, 0, :] = np.where(allowed, bd, 32.0)
    idx[:, 1, :] = _t5_bucket(k - 128 - q).astype(np.float32)
    ident = np.eye(128, dtype=np.float32).astype(ml_dtypes.bfloat16)
    s = np.arange(128)[:, None]
    t = np.arange(128)[None, :]
    tri_p = np.where(s <= t, -1.0 / 16.0, 0.0).astype(np.float32)
    tri_s = np.where((s <= t) & ((s // 64) == (t // 64)), -1.0 / 16.0, 0.0).astype(np.float32)
    cm_p = np.repeat((s <= t).astype(np.float32)[:, None, :], 4, axis=1).astype(ml_dtypes.bfloat16)
    cm_s = np.repeat(((s <= t) & ((s // 64) == (t // 64))).astype(np.float32)[:, None, :], 4, axis=1).astype(ml_dtypes.bfloat16)
    sc = np.zeros((128, 12, 3), np.float32)
    for si, tt in enumerate(range(-4, 8)):
        r = tt - core
        if r > 0:
            sc[:, si, 2] = 1.0
        elif r == 0:
            sc[:, si, 0] = 1.0
        elif r == -1:
            sc[:, si, 1] = 1.0
    sel = np.zeros((128, 8), np.float32)
    sel[:, core] = 1.0
    zm = np.zeros((128, 2, 128), np.float32)
    zm[:, 0, :64] = 1.0
    zm[:, 1, 64:] = 1.0
    return dict(c_idx=idx, c_ident=ident, c_trip=tri_p, c_tris=tri_s, c_cmp=cm_p, c_cms=cm_s,
                c_slot=sc.reshape(128, 36), c_sel=sel, c_zm=zm.astype(ml_dtypes.bfloat16))


GT = 4
VW = 130
NSEQ = 16
STOP = 99
DEBUG = False


class StopBuild(Exception):
    pass


def _stop(level):
    if STOP <= level:
        raise StopBuild()


def build(NPT, PB, L):
    SEQ = NPT * 128
    NROW = SEQ + NSEQ * 64
    P = PB * 128
    nc = bass.Bass("TRN2", target_bir_lowering=False)
    S = Sched()
    st = ExitStack()

    def din(name, shape, dt=F32):
        return nc.dram_tensor(name, list(shape), dt, kind="ExternalInput")

    def dout(name, shape, dt=F32):
        return nc.dram_tensor(name, list(shape), dt, kind="ExternalOutput")

    xin = din("xin", [NROW, D])
    w_in = din("w_in", [L, D, DIN])
    w_out = din("w_out", [L, D, D])
    w_f1 = din("w_f1", [L, D, 2 * DFF])
    w_f2 = din("w_f2", [L, DFF, D])
    w2a = din("w2a", [L, 17, 256])
    g_mix = din("g_mix", [L, D])
    g_ffn = din("g_ffn", [L, D])
    g_fin = din("g_fin", [1, D])
    g_gla = din("g_gla", [L, 128])
    g_dif = din("g_dif", [L, 128])
    lam_in = din("lam_in", [L, 256])
    rb_in = din("rb_in", [1, 128])
    ck = din("ck", [L, NSEQ, 4, 128, P])
    cv = din("cv", [L, NSEQ, 4, 128, PB * 128])
    sg = din("sg", [L, NSEQ, 256, 128])
    c_idx = din("c_idx", [128, 2, 128])
    c_ident = din("c_ident", [128, 128], BF16)
    c_trip = din("c_trip", [128, 128])
    c_cmp = din("c_cmp", [128, 4, 128], BF16)

    y_o = dout("y_o", [NROW, D])
    nk_o = dout("nk_o", [L, NROW, 512])
    nv_o = dout("nv_o", [L, NROW, 512])
    ngp_o = dout("ngp_o", [L, 256, 128])
    ngs_o = dout("ngs_o", [L, NSEQ, 256, 128])
    dbg_o = dout("dbg_o", [2, 128, 8 * GT * 128], BF16) if DEBUG else None
    dbgx_o = dout("dbgx_o", [3, 128, GT, D]) if DEBUG else None

    KTd = [nc.dram_tensor("KTd%d" % l, [4, 128, NPT * 128], BF16) for l in range(L)]
    Vd = [nc.dram_tensor("Vd%d" % l, [4, 128, NPT, VW], BF16) for l in range(L)]

    def sb(name, shape, dt=F32):
        return st.enter_context(nc.sbuf_tensor(name, list(shape), dt))

    NG1 = 1552
    NG2 = 1536
    XTG_N = KC * GT * 128
    ACTB_N = NFF * GT * 128
    CKB_OFF = XTG_N + ACTB_N
    CVB_OFF = CKB_OFF + P
    RKN = max(NPT * 128 + NPT * VW, CVB_OFF + PB * VW)
    RWN = max(KC * NG1, NFF * 512, 3 * KC * 512)

    X = sb("X", [128, GT, D])
    QT = sb("QT", [128, 4, 2, GT * 128], BF16)
    MT = sb("MT", [128, 8, GT * 128], BF16)
    RW = sb("RW", [128, RWN], BF16)
    RK = sb("RK", [128, RKN], BF16)
    BMS = sb("BMS", [128, 2, 4, 128], BF16)
    GB = sb("GB", [128, D])
    ident = sb("ident", [128, 128], BF16)
    ident2 = sb("ident2", [128, 2, 128], BF16)
    ident2s = sb("ident2s", [64, 2, 64], BF16)
    trip = sb("trip", [128, 128])
    cmp_ = sb("cmp", [128, 4, 128], BF16)
    rbb = sb("rbb", [128, 128])
    ch = sb("ch", [128, 4])
    lamv = sb("lamv", [128, L])
    nlam = sb("nlam", [128, L])
    ggl4 = sb("ggl4", [128, L, 4, 128])
    gdf = sb("gdf", [128, L * 128])
    w2s = sb("w2s", [17, L * 256], BF16)
    xn = sb("xn", [128, D], BF16)
    junk = sb("junk", [128, D])
    ssq = sb("ssq", [128, 8])
    rstd = sb("rstd", [128, 8])
    glrT = sb("glrT", [17, 128], BF16)
    lsp = sb("lsp", [128, 256])
    ebT = sb("ebT", [64, 4, 128])
    enbT = sb("enbT", [64, 4, 128])
    enb = sb("enb", [128, 256])
    ktT = sb("ktT", [64, 4, 128], BF16)
    qtT = sb("qtT", [64, 4, 128], BF16)
    ktok = sb("ktok", [128, 256], BF16)
    vtok = sb("vtok", [128, 512], BF16)
    scT = sb("scT", [128, 4, 128], BF16)
    gsl = sb("gsl", [128, 512])
    mixb = sb("mixb", [128, 512], BF16)
    udt = sb("udt", [64, 4, 128])
    Sst = sb("Sst", [64, L, 4, 128])
    Sb = sb("Sb", [64, L, 4, 128], BF16)
    S0 = sb("S0", [64, 4, 128])
    S0b = sb("S0b", [64, 4, 128], BF16)
    ev32 = [sb("ev32_%d" % i, [128, 512]) for i in range(2)]
    vaug = sb("vaug", [128, 4, VW], BF16)
    kts = sb("kts", [128, 4, 128], BF16)
    eT = [sb("eT%d" % i, [128, 1024], BF16) for i in range(2)]
    osb = sb("osb", [128, 2, VW])
    rs = sb("rs", [128, 4])
    od = sb("od", [128, 128])
    odb = sb("odb", [128, 128], BF16)
    ktsS = sb("ktsS", [128, GT, 4, 64], BF16)
    vaugS = sb("vaugS", [64, GT, 4, VW], BF16)

    PSL = st.enter_context(nc.psum_tensor("PSL", [128, 2048], F32))
    ACC = st.enter_context(nc.psum_tensor("ACC", [128, 1536], F32))
    PT = st.enter_context(nc.psum_tensor("PT", [128, 1024], BF16))
    BANK = [PSL[:, q * 512:(q + 1) * 512] for q in range(4)] + [ACC[:, q * 512:(q + 1) * 512] for q in range(3)]
    ACCKEYS = [("bank", 4), ("bank", 5), ("bank", 6)]
    psrr = [0]

    def nextps():
        i = psrr[0]
        psrr[0] = (i + 1) % 7
        return i

    def acc_ap(a, NQ):
        return ACC[0:NQ, (a // 3) * 512 + (a % 3) * VW:(a // 3) * 512 + (a % 3 + 1) * VW], a // 3

    idxv = RK[:, 0:512].bitcast(F32).rearrange("p (a k) -> p a k", a=2)
    tmpb = RK[:, 512:1024].bitcast(F32)
    lamt = RK[:, 1024:1024 + L * 512].bitcast(F32)
    b2o = 1024 + L * 512
    b2 = RK[:, b2o:b2o + 2048].bitcast(F32).rearrange("p (a h k) -> p a h k", a=2, h=4)
    gtmp = RK[:, b2o + 2048:b2o + 2048 + L * 256].bitcast(F32)

    XTg = RK[:, 0:XTG_N].rearrange("p (k t) -> p k t", k=KC)
    ACTB = RK[:, XTG_N:XTG_N + ACTB_N].rearrange("p (j t) -> p j t", j=NFF)
    ckb = RK[:, CKB_OFF:CKB_OFF + P]
    cvb = RK[:, CVB_OFF:CVB_OFF + PB * VW].rearrange("p (j c) -> p j c", c=VW)
    RKALL = ["RK", "actb", "ckb", "cvb"] + [("xtg", t) for t in range(GT)]
    RWALL = ["RW", "w2h", ("slab", 0), ("slab", 1), ("slab", 2)]

    def dma(q, out, in_, r, w):
        return S.add(q, lambda e: e.dma_start(out=out, in_=in_), r, w, dma=True)

    def pe(fn, r, w):
        return S.add("pe", fn, r, w)

    def act(fn, r, w):
        return S.add("act", fn, r, w)

    def dve(fn, r, w):
        return S.add("dve", fn, r, w)

    def rsq(dst, src, scale, rkeys, wkey):
        act(lambda e: e.activation(out=dst, in_=src, func=AF.Sqrt, scale=scale, bias=EPS), rkeys, [wkey])
        dve(lambda e: e.reciprocal(out=dst, in_=dst), [wkey], [wkey])

    def mm(out, lhsT, rhs, start=True, stop=True):
        return lambda e: e.matmul(out, lhsT, rhs, start=start, stop=stop)

    def mm_group(out, pairs):
        n = len(pairs)

        def fn(e):
            ins = None
            for i, (a, b) in enumerate(pairs):
                ins = e.matmul(out, a, b, start=(i == 0), stop=(i == n - 1))
            return ins
        return fn

    dve(lambda e: e.memset(RK[:, :], 0.0), [], RKALL)
    dve(lambda e: e.memset(X[:, :, :], 0.0), [], [("x", t) for t in range(GT)])
    dve(lambda e: e.memset(QT[:, :, :, :], 0.0), [], [("qt", t) for t in range(GT)])
    dma("sp", ident[:, :], c_ident[:, :], [], ["ident"])
    for j in range(2):
        dma("sp", ident2[:, j, :], c_ident[:, :], [], ["ident2"])
        dma("sp", ident2s[:, j, :], c_ident[0:64, 0:64], [], ["ident2s"])
    dma("sp", trip[:, :], c_trip[:, :], [], ["trip"])
    dma("sp", cmp_[:, :, :], c_cmp[:, :, :], [], ["cmp"])
    dma("sp", idxv, c_idx[:, :, :], RKALL, ["idxv"])
    dma("sp", rbb[:, :], rb_in[0:1, :].partition_broadcast(128), [], ["rbb"])
    dma("sp", lamt, lam_in[:, :].rearrange("l n -> (l n)").partition_broadcast(128), RKALL, ["lamt"])
    dma("sp", gtmp, g_gla[:, :].rearrange("l n -> (l n)").partition_broadcast(128), RKALL, ["gtmp"])
    dma("sp", gdf[:, :], g_dif[:, :].rearrange("l n -> (l n)").partition_broadcast(128), [], ["gdf"])
    dma("pool", w2s[:, :].rearrange("p (l n) -> p l n", l=L), w2a[:, :, :].rearrange("l p n -> p l n"), [], ["w2s"])
    dve(lambda e: e.memset(glrT[:, :], 1.0), [], ["glrT"])
    dve(lambda e: e.memset(vaug[:, :, :], 1.0), [], ["vaug"])
    dve(lambda e: e.memset(Sst[:, :, :, :], 0.0), [], ["Sst"])
    dve(lambda e: e.memset(Sb[:, :, :, :], 0.0), [], ["Sb"])
    for l in range(L):
        for h in range(4):
            dve(lambda e, l=l, h=h: e.tensor_copy(out=ggl4[:, l, h, :], in_=gtmp[:, l * 128:(l + 1) * 128]),
                ["gtmp"], ["ggl4"])

    for l in range(L):
        lam_init = 0.8 - 0.6 * math.exp(-0.3 * l)
        for j in range(2):
            a = lamt[:, l * 256 + j * 128: l * 256 + j * 128 + 64]
            b = lamt[:, l * 256 + j * 128 + 64: l * 256 + j * 128 + 128]
            dve(lambda e, a=a, b=b, j=j: e.tensor_tensor(out=tmpb[:, j * 64:(j + 1) * 64], in0=a, in1=b, op=ALU.mult),
                ["lamt"], [("tmpb", j)])
            dve(lambda e, j=j: e.tensor_reduce(out=ssq[:, j:j + 1], in_=tmpb[:, j * 64:(j + 1) * 64],
                                                axis=mybir.AxisListType.X, op=ALU.add), [("tmpb", j)], [("ssq", j)])
        act(lambda e: e.activation(out=rstd[:, 0:2], in_=ssq[:, 0:2], func=AF.Exp), [("ssq", 0), ("ssq", 1)], ["rstd01"])
        dve(lambda e, l=l: e.tensor_tensor(out=lamv[:, l:l + 1], in0=rstd[:, 0:1], in1=rstd[:, 1:2], op=ALU.subtract),
            ["rstd01"], [("lamv", l)])
        dve(lambda e, l=l, li=lam_init: e.tensor_scalar(out=nlam[:, l:l + 1], in0=lamv[:, l:l + 1], scalar1=li,
                                                        scalar2=-1.0, op0=ALU.add, op1=ALU.mult),
            [("lamv", l)], [("nlam", l)])
        dve(lambda e, l=l, li=lam_init: e.tensor_scalar(out=gdf[:, l * 128:(l + 1) * 128], in0=gdf[:, l * 128:(l + 1) * 128],
                                                        scalar1=1.0 - li, scalar2=None, op0=ALU.mult), ["gdf"], ["gdf"])

    for h in range(4):
        dve(lambda e, h=h: e.tensor_copy(out=ch[:, h:h + 1], in_=rbb[:, 15 * 4 + h:15 * 4 + h + 1]), ["rbb"], [("ch", h)])
    for h in range(4):
        for j in range(2):
            dve(lambda e, h=h, j=j: e.memset(b2[:, j, h, :], 0.0), [], [("b2h", h, j)])
            for bk in range(33):
                if bk < 32:
                    sc_ap = rbb[:, bk * 4 + h: bk * 4 + h + 1]
                    dve(lambda e, j=j, bk=bk, sc_ap=sc_ap: e.tensor_scalar(
                        out=tmpb[:, 0:128], in0=idxv[:, j, :], scalar1=float(bk), scalar2=sc_ap,
                        op0=ALU.is_equal, op1=ALU.mult), ["idxv", "rbb"], ["tmpb0"])
                else:
                    dve(lambda e, j=j: e.tensor_scalar(
                        out=tmpb[:, 0:128], in0=idxv[:, j, :], scalar1=32.0, scalar2=NEG,
                        op0=ALU.is_equal, op1=ALU.mult), ["idxv"], ["tmpb0"])
                dve(lambda e, h=h, j=j: e.tensor_tensor(out=b2[:, j, h, :], in0=b2[:, j, h, :], in1=tmpb[:, 0:128],
                                                        op=ALU.add), ["tmpb0", ("b2h", h, j)], [("b2h", h, j)])
            dve(lambda e, h=h, j=j: e.tensor_scalar(out=BMS[:, j, h, :], in0=b2[:, j, h, :], scalar1=ch[:, h:h + 1],
                                                    scalar2=None, op0=ALU.subtract), [("b2h", h, j), ("ch", h)], ["BMS"])
    SETUP_KEYS = ["idxv", "lamt", "gtmp", "tmpb0", ("tmpb", 0), ("tmpb", 1)] + [("b2h", h, j) for h in range(4) for j in range(2)]

    def norm_tile(tl, NR, extra_w):
        act(lambda e: e.activation(out=junk[0:NR, :], in_=X[0:NR, tl, :], func=AF.Square, accum_out=ssq[0:NR, 2:3]),
            [("x", tl)], ["junk", ("ssq", 2)])
        rsq(rstd[0:NR, 2:3], ssq[0:NR, 2:3], 1.0 / D, [("ssq", 2)], ("rstd", 2))
        dve(lambda e: e.scalar_tensor_tensor(out=xn[0:NR, :], in0=X[0:NR, tl, :], scalar=rstd[0:NR, 2:3], in1=GB[0:NR, :],
                                             op0=ALU.mult, op1=ALU.mult), [("x", tl), ("rstd", 2), "GB"], ["xn"])

        def tr(e):
            ins = None
            for k in range(KC):
                ins = e.transpose(PT[:, k * 128:k * 128 + NR], xn[0:NR, k * 128:(k + 1) * 128], ident[0:NR, 0:NR])
            return ins
        pe(tr, ["xn", "ident"], ["PT"])
        dve(lambda e: e.tensor_copy(out=XTg[:, :, tl * 128:tl * 128 + NR],
                                    in_=PT[:, :].rearrange("p (k t) -> p k t", k=KC)[:, :, 0:NR]),
            ["PT"], [("xtg", tl)] + extra_w)

    def tile_gla(l, sample, tl, NR, seq):
        W1 = RW[:, 0:KC * NG1].rearrange("p (k n) -> p k n", k=KC)
        xc = [XTg[:, k, tl * 128:tl * 128 + NR] for k in range(KC)]
        xk = ("xtg", tl)
        if sample:
            dma("sp", S0[:, :, :], sg[l, seq, :, :].rearrange("(h p) v -> p h v", p=64), [], ["S0"])
            dve(lambda e: e.tensor_copy(out=S0b[:, :, :], in_=S0[:, :, :]), ["S0"], ["S0b"])
        p0 = nextps()
        pe(mm_group(BANK[p0][0:16, 0:NR], [(W1[:, k, C_GLR:C_GLR + 16], xc[k]) for k in range(KC)]), [xk, "RW"], [("bank", p0)])
        dve(lambda e: e.tensor_copy(out=glrT[0:16, 0:NR], in_=BANK[p0][0:16, 0:NR]), [("bank", p0)], ["glrT"])
        p1 = nextps()
        pe(mm(BANK[p1][0:NR, 0:256], glrT[:, 0:NR], w2s[:, l * 256:(l + 1) * 256]), ["glrT", "w2s"], [("bank", p1)])
        act(lambda e: e.activation(out=lsp[0:NR, :], in_=BANK[p1][0:NR, 0:256], func=AF.Exp, scale=-1.0),
            [("bank", p1)], ["lsp"])
        act(lambda e: e.activation(out=lsp[0:NR, :], in_=lsp[0:NR, :], func=AF.Ln, bias=1.0), ["lsp"], ["lsp"])
        _stop(1.2)
        p2 = nextps()
        pe(mm(BANK[p2][0:NR, 0:256], trip[0:NR, 0:NR], lsp[0:NR, :]), ["lsp", "trip"], [("bank", p2)])
        act(lambda e: e.activation(out=enb[0:NR, :], in_=BANK[p2][0:NR, 0:256], func=AF.Exp, scale=-1.0),
            [("bank", p2)], ["enb"])
        p3 = nextps()

        def bt(e):
            ins = None
            for h in range(4):
                ins = e.matmul(BANK[p3][0:64, h * 128:h * 128 + NR], lsp[0:NR, h * 64:(h + 1) * 64], trip[0:NR, 0:NR],
                               start=True, stop=True)
            return ins
        pe(bt, ["lsp", "trip"], [("bank", p3)])
        p3v = BANK[p3][0:64, :].rearrange("p (h t) -> p h t", h=4)[:, :, 0:NR]
        act(lambda e: e.activation(out=ebT[:, :, 0:NR], in_=p3v, func=AF.Exp), [("bank", p3)], ["ebT"])
        act(lambda e: e.activation(out=enbT[:, :, 0:NR], in_=p3v, func=AF.Exp, scale=-1.0), [("bank", p3)], ["enbT"])
        _stop(1.3)
        p4 = nextps()
        pe(mm_group(BANK[p4][0:NR, 0:256], [(xc[k], W1[:, k, C_GK:C_GK + 256]) for k in range(KC)]), [xk, "RW"],
           [("bank", p4)])
        dve(lambda e: e.tensor_tensor(out=ktok[0:NR, :], in0=BANK[p4][0:NR, 0:256], in1=enb[0:NR, :], op=ALU.mult),
            [("bank", p4), "enb"], ["ktok"])
        p5 = nextps()
        pe(mm_group(BANK[p5][0:NR, :], [(xc[k], W1[:, k, C_GV:C_GV + 512]) for k in range(KC)]), [xk, "RW"], [("bank", p5)])
        act(lambda e: e.activation(out=vtok[0:NR, :], in_=BANK[p5][0:NR, :], func=AF.Copy), [("bank", p5)], ["vtok"])
        _stop(1.4)
        p6 = nextps()
        p7 = nextps()

        def kq(e):
            ins = None
            for pb, c0 in ((p6, C_GK), (p7, C_GQ)):
                for h in range(4):
                    for k in range(KC):
                        ins = e.matmul(BANK[pb][0:64, h * 128:h * 128 + NR], W1[:, k, c0 + h * 64:c0 + (h + 1) * 64], xc[k],
                                       start=(k == 0), stop=(k == KC - 1))
            return ins
        pe(kq, [xk, "RW"], [("bank", p6), ("bank", p7)])
        p6v = BANK[p6][0:64, :].rearrange("p (h t) -> p h t", h=4)[:, :, 0:NR]
        p7v = BANK[p7][0:64, :].rearrange("p (h t) -> p h t", h=4)[:, :, 0:NR]
        dve(lambda e: e.tensor_tensor(out=ktT[:, :, 0:NR], in0=p6v, in1=enbT[:, :, 0:NR], op=ALU.mult),
            [("bank", p6), "enbT"], ["ktT"])
        dve(lambda e: e.scalar_tensor_tensor(out=qtT[:, :, 0:NR], in0=p7v, scalar=0.125, in1=ebT[:, :, 0:NR],
                                             op0=ALU.mult, op1=ALU.mult), [("bank", p7), "ebT"], ["qtT"])
        _stop(1.5)
        p8 = nextps()

        def sc(e):
            ins = None
            for h in range(4):
                ins = e.matmul(BANK[p8][0:NR, h * 128:h * 128 + NR], ktT[:, h, 0:NR], qtT[:, h, 0:NR], start=True, stop=True)
            return ins
        pe(sc, ["ktT", "qtT"], [("bank", p8)])
        p8v = BANK[p8][0:NR, :].rearrange("p (h t) -> p h t", h=4)[:, :, 0:NR]
        dve(lambda e: e.tensor_tensor(out=scT[0:NR, :, 0:NR], in0=p8v, in1=cmp_[0:NR, :, 0:NR], op=ALU.mult),
            [("bank", p8), "cmp"], ["scT"])
        _stop(1.6)
        p9 = nextps()

        def oo(e):
            ins = None
            for h in range(4):
                out = BANK[p9][0:NR, h * 128:(h + 1) * 128]
                e.matmul(out, scT[0:NR, h, 0:NR], vtok[0:NR, h * 128:(h + 1) * 128], start=True, stop=False)
                sbh = S0b[:, h, :] if sample else Sb[:, l, h, :]
                ins = e.matmul(out, qtT[:, h, 0:NR], sbh, start=False, stop=True)
            return ins
        pe(oo, ["scT", "vtok", "qtT", "S0b", "Sb"], [("bank", p9)])
        _stop(1.7)
        pu = nextps()

        def uu(e):
            ins = None
            for h in range(4):
                ins = e.matmul(BANK[pu][0:64, h * 128:(h + 1) * 128], ktok[0:NR, h * 64:(h + 1) * 64],
                               vtok[0:NR, h * 128:(h + 1) * 128], start=True, stop=True)
            return ins
        pe(uu, ["ktok", "vtok"], [("bank", pu)])
        for h in range(4):
            dcol = ebT[:, h, NR - 1:NR]
            dve(lambda e, h=h, dcol=dcol: e.tensor_scalar(out=udt[:, h, :], in0=BANK[pu][0:64, h * 128:(h + 1) * 128],
                                                          scalar1=dcol, scalar2=None, op0=ALU.mult),
                [("bank", pu), "ebT"], [("udt", h)])
            if sample:
                dve(lambda e, h=h, dcol=dcol: e.scalar_tensor_tensor(out=udt[:, h, :], in0=S0[:, h, :], scalar=dcol,
                                                                     in1=udt[:, h, :], op0=ALU.mult, op1=ALU.add),
                    ["S0", "ebT", ("udt", h)], [("udt", h)])
            else:
                dve(lambda e, h=h, dcol=dcol: e.scalar_tensor_tensor(out=Sst[:, l, h, :], in0=Sst[:, l, h, :], scalar=dcol,
                                                                     in1=udt[:, h, :], op0=ALU.mult, op1=ALU.add),
                    ["Sst", "ebT", ("udt", h)], ["Sst"])
        if sample:
            dma("sp", ngs_o[l, seq, :, :].rearrange("(h p) v -> p h v", p=64), udt[:, :, :],
                [("udt", h) for h in range(4)], [])
        else:
            dve(lambda e: e.tensor_copy(out=Sb[:, l, :, :], in_=Sst[:, l, :, :]), ["Sst"], ["Sb"])
        _stop(1.8)
        pg = nextps()
        pe(mm_group(BANK[pg][0:NR, :], [(xc[k], W1[:, k, C_GG:C_GG + 512]) for k in range(KC)]), [xk, "RW"], [("bank", pg)])
        act(lambda e: e.activation(out=gsl[0:NR, :], in_=BANK[pg][0:NR, :], func=AF.Silu), [("bank", pg)], ["gsl"])
        dve(lambda e: e.tensor_tensor(out=gsl[0:NR, :], in0=gsl[0:NR, :],
                                      in1=ggl4[0:NR, l, :, :].rearrange("p h v -> p (h v)"), op=ALU.mult),
            ["gsl", "ggl4"], ["gsl"])
        for h in range(4):
            act(lambda e, h=h: e.activation(out=junk[0:NR, h * 128:(h + 1) * 128], in_=BANK[p9][0:NR, h * 128:(h + 1) * 128],
                                            func=AF.Square, accum_out=ssq[0:NR, 4 + h:5 + h]),
                [("bank", p9)], ["junk", ("ssq", 4 + h)])
        rsq(rstd[0:NR, 4:8], ssq[0:NR, 4:8], 1.0 / 128, [("ssq", 4 + h) for h in range(4)], "rstd4")
        for h in range(4):
            dve(lambda e, h=h: e.scalar_tensor_tensor(out=mixb[0:NR, h * 128:(h + 1) * 128],
                                                      in0=BANK[p9][0:NR, h * 128:(h + 1) * 128],
                                                      scalar=rstd[0:NR, 4 + h:5 + h], in1=gsl[0:NR, h * 128:(h + 1) * 128],
                                                      op0=ALU.mult, op1=ALU.mult),
                [("bank", p9), "rstd4", "gsl"], ["mixb"])

        def tr(e):
            ins = None
            for h in range(4):
                ins = e.transpose(PT[:, h * 128:h * 128 + NR], mixb[0:NR, h * 128:(h + 1) * 128], ident[0:NR, 0:NR])
            return ins
        pe(tr, ["mixb", "ident"], ["PT"])
        dve(lambda e: e.tensor_copy(out=MT[:, 0:4, tl * 128:tl * 128 + NR],
                                    in_=PT[:, 0:512].rearrange("p (h t) -> p h t", h=4)[:, :, 0:NR]), ["PT"], [("mt", tl)])
        _stop(1.9)

    def tile_diff(l, sample, tl, NR, row0, gt):
        W2 = RW[:, 0:KC * NG2].rearrange("p (k n) -> p k n", k=KC)
        xc = [XTg[:, k, tl * 128:tl * 128 + NR] for k in range(KC)]
        xk = ("xtg", tl)
        pq = nextps()
        pk = nextps()

        def dqk(e):
            ins = None
            for pb, c0 in ((pq, 0), (pk, 512)):
                for h in range(4):
                    for k in range(KC):
                        ins = e.matmul(BANK[pb][:, h * 128:h * 128 + NR], W2[:, k, c0 + h * 128:c0 + (h + 1) * 128], xc[k],
                                       start=(k == 0), stop=(k == KC - 1))
            return ins
        pe(dqk, [xk, "RW"], [("bank", pq), ("bank", pk)])
        pqv = BANK[pq][:, :].rearrange("p (h t) -> p h t", h=4)
        for m in range(2):
            rows = slice(m * 64, m * 64 + 64)
            act(lambda e, m=m, rows=rows: e.activation(out=QT[rows, :, m, tl * 128:tl * 128 + NR], in_=pqv[rows, :, 0:NR],
                                                       func=AF.Copy, scale=0.125), [("bank", pq)], [("qt", tl)])
        act(lambda e: e.activation(out=kts[:, :, 0:NR], in_=BANK[pk][:, :].rearrange("p (h t) -> p h t", h=4)[:, :, 0:NR],
                                   func=AF.Copy), [("bank", pk)], ["kts"])
        _stop(2.1)
        if not sample:
            dma("sp", KTd[l][:, :, gt * 128:(gt + 1) * 128].rearrange("h d t -> d h t"), kts[:, :, :], ["kts"], [("ktd", l)])
        else:
            dve(lambda e: e.tensor_copy(out=ktsS[:, tl, :, :], in_=kts[:, :, 0:64]), ["kts"], [("ktsa", tl)])
        _stop(2.2)
        for which, coff, dst in ((0, 512, nk_o), (1, 1024, nv_o)):
            pp = nextps()
            pe(mm_group(BANK[pp][0:NR, :], [(xc[k], W2[:, k, coff:coff + 512]) for k in range(KC)]), [xk, "RW"],
               [("bank", pp)])
            dve(lambda e, pp=pp, which=which: e.tensor_copy(out=ev32[which][0:NR, :], in_=BANK[pp][0:NR, :]),
                [("bank", pp)], [("ev", which)])
            dma("sp", dst[l, row0:row0 + NR, :], ev32[which][0:NR, :], [("ev", which)], [])
            _stop(2.3)
            if which == 1:
                act(lambda e: e.activation(out=vaug[0:NR, :, 0:128],
                                           in_=ev32[1][0:NR, :].rearrange("p (h v) -> p h v", h=4), func=AF.Copy),
                    [("ev", 1)], ["vaug"])
                _stop(2.35)
                if not sample:
                    dma("sp", Vd[l][:, :, gt, :].rearrange("h t c -> t h c"), vaug[:, :, :], ["vaug"], [("vd", l)])
                else:
                    dve(lambda e: e.tensor_copy(out=vaugS[:, tl, :, :], in_=vaug[0:64, :, :]), ["vaug"], [("vauga", tl)])
                _stop(2.4)

    chunk_i = [0]

    def attend(l, tl, h, NQ, blocks, rkeys, a0, started, last):
        qc = slice(tl * 128, tl * 128 + NQ)
        CW = 2 * NQ
        CB = 1024 // CW
        I2 = (ident2[:, :, :] if NQ == 128 else ident2s[:, :, :]).rearrange("p a q -> p (a q)")
        chunks = []
        cur = []
        for bi, blk in enumerate(blocks):
            if cur and (len(cur) == CB or blk[2] != cur[0][1][2]):
                chunks.append(cur)
                cur = []
            cur.append((bi, blk))
        if cur:
            chunks.append(cur)
        nbt = len(blocks)
        for chn in chunks:
            c = chunk_i[0] % 2
            chunk_i[0] += 1
            base = c * 1024
            nk = chn[0][1][2]
            bk = [("bank", 2 * c), ("bank", 2 * c + 1)]

            def qk(e, chn=chn, base=base):
                ins = None
                for ci, (bi, (lhsT, v, nk_, bias)) in enumerate(chn):
                    o0 = base + ci * CW
                    if bias is not None:
                        e.matmul(PSL[0:nk_, o0:o0 + CW], bias, I2, start=True, stop=False)
                    for m in range(2):
                        ins = e.matmul(PSL[0:nk_, o0 + m * NQ:o0 + (m + 1) * NQ], lhsT, QT[:, h, m, qc],
                                       start=(bias is None), stop=True)
                return ins
            pe(qk, rkeys + [("qt", tl), "BMS", "ident2", "ident2s"], bk)
            n = len(chn)
            act(lambda e, c=c, base=base, n=n, nk=nk: e.activation(
                out=eT[c][0:nk, 0:n * CW], in_=PSL[0:nk, base:base + n * CW], func=AF.Exp, bias=ch[0:nk, h:h + 1]),
                bk + [("ch", h)], [("eT", c)])

            flags = []
            for ci, (bi, blk_) in enumerate(chn):
                for m in range(2):
                    ap_, bnk = acc_ap(a0 + m, NQ)
                    flags.append(bnk not in started)
                    started.add(bnk)

            def av(e, chn=chn, c=c, flags=flags):
                ins = None
                fi = 0
                for ci, (bi, (lhsT, v, nk_, bias)) in enumerate(chn):
                    for m in range(2):
                        ap_, bnk = acc_ap(a0 + m, NQ)
                        ins = e.matmul(ap_, eT[c][0:nk_, ci * CW + m * NQ:ci * CW + (m + 1) * NQ], v,
                                       start=flags[fi], stop=(last and bi == nbt - 1), skip_group_check=True)
                        fi += 1
                return ins
            pe(av, [("eT", c)] + rkeys, ACCKEYS)
        if last:
            finalize(l, tl, h, NQ, a0)

    def finalize(l, tl, h, NQ, a0):
        R = slice(0, NQ)
        for m in range(2):
            ap_, bnk = acc_ap(a0 + m, NQ)
            dve(lambda e, m=m, ap_=ap_: e.tensor_copy(out=osb[R, m, :], in_=ap_), ACCKEYS, [("osb", m)])
        dve(lambda e: e.reciprocal(out=rs[R, 0:2], in_=osb[R, :, 128]), [("osb", 0), ("osb", 1)], ["rs"])
        dve(lambda e: e.tensor_tensor(out=rs[R, 1:2], in0=rs[R, 1:2], in1=nlam[R, l:l + 1], op=ALU.mult),
            ["rs", ("nlam", l)], ["rs"])
        dve(lambda e: e.tensor_scalar(out=od[R, :], in0=osb[R, 0, 0:128], scalar1=rs[R, 0:1], scalar2=None,
                                      op0=ALU.mult), [("osb", 0), "rs"], ["od"])
        dve(lambda e: e.scalar_tensor_tensor(out=od[R, :], in0=osb[R, 1, 0:128], scalar=rs[R, 1:2],
                                             in1=od[R, :], op0=ALU.mult, op1=ALU.add), [("osb", 1), "rs", "od"], ["od"])
        act(lambda e: e.activation(out=junk[R, 0:128], in_=od[R, :], func=AF.Square, accum_out=ssq[R, 3:4]),
            ["od"], ["junk", ("ssq", 3)])
        rsq(rstd[R, 3:4], ssq[R, 3:4], 1.0 / 128, [("ssq", 3)], ("rstd", 3))
        dve(lambda e: e.scalar_tensor_tensor(out=odb[R, :], in0=od[R, :], scalar=rstd[R, 3:4],
                                             in1=gdf[R, l * 128:(l + 1) * 128], op0=ALU.mult, op1=ALU.mult),
            ["od", ("rstd", 3), "gdf"], ["odb"])
        pe(lambda e: e.transpose(PT[:, 0:NQ], odb[R, :], ident[R, R]), ["odb", "ident"], ["PT"])
        dve(lambda e: e.tensor_copy(out=MT[:, 4 + h, tl * 128:tl * 128 + NQ], in_=PT[:, 0:NQ]), ["PT"], [("mt", tl)])

    def _main(groups):
        first_sample = [True]
        for kind, g in groups:
            sample = (kind == "s")
            NR = 64 if sample else 128
            if sample:
                r0 = SEQ + g * GT * 64
                dma("sp", X[0:64, :, :], xin[r0:r0 + GT * 64, :].rearrange("(t p) d -> p t d", p=64),
                    [], [("x", t) for t in range(GT)])
            else:
                r0 = g * GT * 128
                dma("sp", X[:, :, :], xin[r0:r0 + GT * 128, :].rearrange("(t p) d -> p t d", p=128),
                    [], [("x", t) for t in range(GT)])
            for l in range(L):
                dma("sp", GB[:, :], g_mix[l:l + 1, :].partition_broadcast(128), [], ["GB"])
                _stop(1)
                for tl in range(GT):
                    norm_tile(tl, NR, SETUP_KEYS if (g == 0 and l == 0 and not sample) else [])
                _stop(1.1)
                dma("pool", RW[:, 0:KC * NG1].rearrange("p (k n) -> p k n", k=KC),
                    w_in[l, :, 0:NG1].rearrange("(k p) n -> p k n", p=128), [], RWALL)
                for tl in range(GT):
                    tile_gla(l, sample, tl, NR, g * GT + tl)
                _stop(2)
                dma("pool", RW[:, 0:KC * NG2].rearrange("p (k n) -> p k n", k=KC),
                    w_in[l, :, NG1:DIN].rearrange("(k p) n -> p k n", p=128), [], RWALL)
                for tl in range(GT):
                    tile_diff(l, sample, tl, NR, r0 + tl * NR, g * GT + tl)
                _stop(2.5)
                if not sample:
                    gt0 = g * GT
                    NKB = gt0 + GT
                    KTh = RK[:, 0:NKB * 128]
                    Vh = RK[:, NPT * 128:NPT * 128 + NKB * VW].rearrange("p (j c) -> p j c", c=VW)
                    for h in range(4):
                        dma("sp", KTh, KTd[l][h, :, 0:NKB * 128], [("ktd", l)], RKALL)
                        dma("sp", Vh, Vd[l][h, :, 0:NKB, :],
                            [("vd", l)], RKALL)
                        started = set()
                        nfar = max(0, gt0 - 1)
                        for j in range(nfar):
                            c = chunk_i[0] % 2
                            chunk_i[0] += 1
                            base = c * 1024
                            bk = [("bank", 2 * c), ("bank", 2 * c + 1)]

                            def qkf(e, j=j, base=base, h=h, KTh=KTh):
                                ins = None
                                for m in range(2):
                                    ins = e.matmul(PSL[:, base + m * 512:base + (m + 1) * 512], KTh[:, j * 128:(j + 1) * 128],
                                                   QT[:, h, m, :], start=True, stop=True)
                                return ins
                            pe(qkf, ["RK"] + [("qt", t) for t in range(GT)], bk)
                            act(lambda e, c=c, base=base, h=h: e.activation(
                                out=eT[c][:, :], in_=PSL[:, base:base + 1024], func=AF.Exp, bias=ch[:, h:h + 1]),
                                bk + [("ch", h)], [("eT", c)])
                            flags = []
                            for tl in range(GT):
                                for m in range(2):
                                    ap_, bnk = acc_ap(tl * 2 + m, 128)
                                    flags.append(bnk not in started)
                                    started.add(bnk)

                            def avf(e, j=j, c=c, flags=flags, Vh=Vh):
                                ins = None
                                fi = 0
                                for tl in range(GT):
                                    for m in range(2):
                                        ap_, bnk = acc_ap(tl * 2 + m, 128)
                                        ins = e.matmul(ap_, eT[c][:, m * 512 + tl * 128:m * 512 + (tl + 1) * 128], Vh[:, j, :],
                                                       start=flags[fi], stop=False, skip_group_check=True)
                                        fi += 1
                                return ins
                            pe(avf, [("eT", c), "RK"], ACCKEYS)
                        for tl in range(GT):
                            i = gt0 + tl
                            blocks = []
                            for j in range(nfar, i + 1):
                                bias = None
                                if j == i:
                                    bias = BMS[:, 0, h, :]
                                elif j == i - 1:
                                    bias = BMS[:, 1, h, :]
                                blocks.append((KTh[:, j * 128:(j + 1) * 128], Vh[:, j, :], 128, bias))
                            attend(l, tl, h, 128, blocks, ["RK"], tl * 2, started, True)
                else:
                    for tl in range(GT):
                        seq = g * GT + tl
                        for h in range(4):
                            dma("pool", ckb, ck[l, seq, h, :, :], [], ["ckb"] + (RKALL if first_sample[0] else []))
                            dma("pool", cvb[:, :, 0:128], cv[l, seq, h, :, :].rearrange("t (j v) -> t j v", v=128), [],
                                ["cvb"] + (RKALL if first_sample[0] else []))
                            if first_sample[0]:
                                dve(lambda e: e.memset(cvb[:, :, 128:VW], 1.0), [], ["cvb"])
                            first_sample[0] = False
                            blocks = []
                            for j in range(PB):
                                bias = BMS[0:64, 1, h, :] if j == PB - 1 else None
                                blocks.append((ckb[:, j * 128:(j + 1) * 128], cvb[:, j, :], 128, bias))
                            blocks.append((ktsS[:, tl, h, :], vaugS[:, tl, h, :], 64, BMS[0:64, 0, h, 0:64]))
                            attend(l, tl, h, 64, blocks, ["ckb", "cvb", ("ktsa", tl), ("vauga", tl)], 0, set(), True)
                _stop(3)
                if DEBUG and l == 0 and g == 0:
                    dma("sp", dbg_o[1 if sample else 0, :, :], MT[:, :, :].rearrange("p c t -> p (c t)"),
                        [("mt", t) for t in range(GT)], [])
                WOv = RW[:, 0:8 * D].rearrange("p (k n) -> p k n", k=8)
                dma("pool", WOv, w_out[l, :, :].rearrange("(k p) n -> p k n", p=128), [], RWALL)
                for tl in range(GT):
                    for half in range(2):
                        pa = nextps()
                        pe(mm_group(BANK[pa][0:NR, :], [(MT[:, cch, tl * 128:tl * 128 + NR], WOv[:, cch, half * 512:(half + 1) * 512])
                                                        for cch in range(8)]), [("mt", tl), "RW"], [("bank", pa)])
                        if DEBUG and l == 0 and g == 0 and not sample and tl == 0 and half == 0:
                            dve(lambda e, pa=pa: e.tensor_copy(out=ev32[0][:, :], in_=BANK[pa][:, :]), [("bank", pa)], [("ev", 0)])
                            dma("sp", dbgx_o[2, :, 0, 0:512], ev32[0][:, :], [("ev", 0)], [])
                        dve(lambda e, pa=pa, half=half, tl=tl, NR=NR: e.tensor_tensor(
                            out=X[0:NR, tl, half * 512:(half + 1) * 512], in0=BANK[pa][0:NR, :],
                            in1=X[0:NR, tl, half * 512:(half + 1) * 512], op=ALU.add), [("bank", pa), ("x", tl)], [("x", tl)])
                _stop(4)
                if DEBUG and l == 0 and g == 0 and not sample:
                    dma("sp", dbgx_o[0, :, :, :], X[:, :, :], [("x", t) for t in range(GT)], [])
                dma("sp", GB[:, :], g_ffn[l:l + 1, :].partition_broadcast(128), [], ["GB"])
                for tl in range(GT):
                    norm_tile(tl, NR, ["RK", "ckb", "cvb"])
                W2h = RW[:, 0:NFF * 512].rearrange("p (j n) -> p j n", j=NFF)
                NTK = GT * 128
                xr = [("xtg", t) for t in range(GT)]
                for jp in range(NFF // 2):
                    sl = jp % 3
                    so = sl * KC * 512
                    slab = RW[:, so:so + KC * 512].rearrange("p (k n) -> p k n", k=KC)
                    dma("pool", slab[:, :, 0:256], w_f1[l, :, jp * 256:(jp + 1) * 256].rearrange("(k p) n -> p k n", p=128),
                        [], ["RW", "w2h", ("slab", sl)])
                    dma("pool", slab[:, :, 256:512],
                        w_f1[l, :, DFF + jp * 256:DFF + (jp + 1) * 256].rearrange("(k p) n -> p k n", p=128), [],
                        ["RW", "w2h", ("slab", sl)])
                    for jj in range(2):
                        j = jp * 2 + jj
                        pg = nextps()
                        pu = nextps()
                        pe(mm_group(BANK[pg][:, 0:NTK], [(slab[:, k, jj * 128:(jj + 1) * 128], XTg[:, k, :]) for k in range(KC)]),
                           [("slab", sl)] + xr, [("bank", pg)])
                        pe(mm_group(BANK[pu][:, 0:NTK], [(slab[:, k, 256 + jj * 128:256 + (jj + 1) * 128], XTg[:, k, :])
                                                         for k in range(KC)]), [("slab", sl)] + xr, [("bank", pu)])
                        act(lambda e, pg=pg: e.activation(out=gsl[:, 0:NTK], in_=BANK[pg][:, 0:NTK], func=AF.Silu),
                            [("bank", pg)], ["gsl"])
                        dve(lambda e, j=j, pu=pu: e.tensor_tensor(out=ACTB[:, j, :], in0=gsl[:, 0:NTK], in1=BANK[pu][:, 0:NTK],
                                                                  op=ALU.mult), ["gsl", ("bank", pu)], ["actb"])
                for half in range(2):
                    dma("pool", W2h, w_f2[l, :, half * 512:(half + 1) * 512].rearrange("(j p) n -> p j n", p=128),
                        [], ["RW", "w2h", ("slab", 0), ("slab", 1), ("slab", 2)])
                    for tl in range(GT):
                        pa = nextps()
                        pe(mm_group(BANK[pa][0:NR, :], [(ACTB[:, j, tl * 128:tl * 128 + NR], W2h[:, j, :]) for j in range(NFF)]),
                           ["actb", "w2h"], [("bank", pa)])
                        dve(lambda e, pa=pa, half=half, tl=tl, NR=NR: e.tensor_tensor(
                            out=X[0:NR, tl, half * 512:(half + 1) * 512], in0=BANK[pa][0:NR, :],
                            in1=X[0:NR, tl, half * 512:(half + 1) * 512], op=ALU.add), [("bank", pa), ("x", tl)], [("x", tl)])
                _stop(5)
                if DEBUG and l == 0 and g == 0 and not sample:
                    dma("sp", dbgx_o[1, :, :, :], X[:, :, :], [("x", t) for t in range(GT)], [])
            dma("sp", GB[:, :], g_fin[0:1, :].partition_broadcast(128), [], ["GB"])
            for tl in range(GT):
                act(lambda e, tl=tl, NR=NR: e.activation(out=junk[0:NR, :], in_=X[0:NR, tl, :], func=AF.Square,
                                                  accum_out=ssq[0:NR, 2:3]), [("x", tl)], ["junk", ("ssq", 2)])
                rsq(rstd[0:NR, 2:3], ssq[0:NR, 2:3], 1.0 / D, [("ssq", 2)], ("rstd", 2))
                dve(lambda e, tl=tl, NR=NR: e.scalar_tensor_tensor(out=junk[0:NR, :], in0=X[0:NR, tl, :], scalar=rstd[0:NR, 2:3],
                                                            in1=GB[0:NR, :], op0=ALU.mult, op1=ALU.mult),
                    [("x", tl), ("rstd", 2), "GB"], ["junk"])
                dma("sp", y_o[r0 + tl * NR:r0 + (tl + 1) * NR, :], junk[0:NR, :], ["junk"], [])
            if kind == "p" and g == NPT // GT - 1:
                for l in range(L):
                    dma("sp", ngp_o[l, :, :].rearrange("(h p) v -> p h v", p=64), Sst[:, l, :, :], ["Sst"], [])

    groups = [("p", g) for g in range(NPT // GT)] + [("s", g) for g in range(NSEQ // GT)]
    try:
        _stop(0)
        _main(groups)
    except StopBuild:
        pass
    S.emit(nc, st)
    st.close()
    return nc


def _run(inp, SEQ, PAST, L):
    NPT = SEQ // 128
    PB = PAST // 128
    nc = build(NPT, PB, L)
    f32 = np.float32
    xin = np.concatenate([np.asarray(inp["x_prompt"], f32).reshape(SEQ, D),
                          np.asarray(inp["x_sample"], f32).reshape(NSEQ * 64, D)], axis=0)
    ckh = np.ascontiguousarray(np.transpose(np.asarray(inp["cache_diff_k"], f32), (0, 1, 3, 4, 2)))
    cvv = np.asarray(inp["cache_diff_v"], f32).reshape(L, NSEQ, PB, 128, 4, 128)
    cvh = np.ascontiguousarray(np.transpose(cvv, (0, 1, 4, 3, 2, 5))).reshape(L, NSEQ, 4, 128, PB * 128)
    w2a = np.concatenate([np.asarray(inp["gla_w_alpha2"], f32), np.asarray(inp["gla_b_alpha"], f32)[:, None, :]], axis=1)
    m = dict(
        xin=xin, w_in=np.asarray(inp["w_in"], f32), w_out=np.asarray(inp["w_out"], f32),
        w_f1=np.asarray(inp["w_ffn_in"], f32), w_f2=np.asarray(inp["w_ffn_out"], f32), w2a=np.ascontiguousarray(w2a),
        g_mix=np.asarray(inp["norm_mix_g"], f32), g_ffn=np.asarray(inp["norm_ffn_g"], f32),
        g_fin=np.asarray(inp["final_norm_g"], f32).reshape(1, D), g_gla=np.asarray(inp["gla_norm_g"], f32),
        g_dif=np.asarray(inp["diff_norm_g"], f32), lam_in=np.asarray(inp["diff_lambda"], f32).reshape(L, 256),
        rb_in=np.asarray(inp["rel_bias"], f32).reshape(1, 128), ck=ckh, cv=cvh,
        sg=np.asarray(inp["state_gla"], f32).reshape(L, NSEQ, 256, 128))
    hc = _host_consts(0)
    m["c_idx"], m["c_ident"], m["c_trip"], m["c_cmp"] = hc["c_idx"], hc["c_ident"], hc["c_trip"], hc["c_cmp"]
    res = run_bass_kernel_spmd(nc, [m], core_ids=[0])
    r = res.results[0]
    global LAST_DBG
    LAST_DBG = r.get("dbg_o")
    global LAST_DBGX
    LAST_DBGX = r.get("dbgx_o")
    y = r["y_o"]
    nk = r["nk_o"]
    nv = r["nv_o"]
    return (y[:SEQ].reshape(1, SEQ, D), y[SEQ:].reshape(NSEQ, 64, D),
            nk[:, :SEQ].reshape(L, 1, SEQ, 4, 128), nv[:, :SEQ].reshape(L, 1, SEQ, 4, 128),
            r["ngp_o"].reshape(L, 1, 4, 64, 128),
            nk[:, SEQ:].reshape(L, NSEQ, 64, 4, 128), nv[:, SEQ:].reshape(L, NSEQ, 64, 4, 128),
            r["ngs_o"].reshape(L, NSEQ, 4, 64, 128))


def kernel(**inputs):
    return _run(inputs, 16384, 4096, 4)
```
